# Optimizing a Trainium2 kernel written in Bass

```python
import jax, jax.numpy as jnp
from jax import lax
import numpy as np

D_MODEL = 2048
BATCH = 4
SEQ = 4096
DEPTH = 2

GRID_W = 64
N_MEM = 256
HEAD_DIM = 128
N_Q_HEADS = D_MODEL // HEAD_DIM
N_KV_HEADS = N_Q_HEADS // 4
GROUP = N_Q_HEADS // N_KV_HEADS
ATTN_W = N_Q_HEADS * HEAD_DIM
KV_W = N_KV_HEADS * HEAD_DIM
ROPE_HALF = HEAD_DIM // 2
ROPE_THETA = 10000.0
Q_BLOCK = 128
D_RNN = D_MODEL
N_RNN_BLOCKS = 16
RNN_BLOCK = D_RNN // N_RNN_BLOCKS
CONV_W = 4
CONV_LEFT = CONV_W // 2
LRU_C = 8.0
N_XHEADS = 4
XHEAD_DIM = D_MODEL // N_XHEADS
D_FF = 4 * D_MODEL
N_IN = ATTN_W + 2 * KV_W + 2 * D_RNN + 2 * D_MODEL
SPLITS = [ATTN_W, ATTN_W + KV_W, ATTN_W + 2 * KV_W, ATTN_W + 2 * KV_W + D_RNN,
          ATTN_W + 2 * KV_W + 2 * D_RNN, ATTN_W + 2 * KV_W + 2 * D_RNN + D_MODEL]
EPS = 1e-6

kernel_name = 'hybrid_gqa_rglru_xattn_encoder'


def rmsnorm(x, g):
    xf = x.astype(jnp.float32)
    y = xf * lax.rsqrt(jnp.mean(xf * xf, axis=-1, keepdims=True) + EPS)
    return (y * g.astype(jnp.float32)).astype(x.dtype)


def axial_rope_tables(seq_len):
    rows_n = seq_len // GRID_W
    row = jnp.repeat(jnp.arange(rows_n, dtype=jnp.float32), GRID_W)
    col = jnp.tile(jnp.arange(GRID_W, dtype=jnp.float32), rows_n)
    n_freq = ROPE_HALF // 2
    inv = ROPE_THETA ** (-jnp.arange(n_freq, dtype=jnp.float32) / n_freq)
    ang_r = row[:, None] * inv[None, :]
    ang_c = col[:, None] * inv[None, :]
    return (jnp.cos(ang_r), jnp.sin(ang_r), jnp.cos(ang_c), jnp.sin(ang_c))


def _rotate(x, cos, sin):
    n = x.shape[-1] // 2
    x1, x2 = x[..., :n], x[..., n:]
    c = cos[None, :, None, :]
    s = sin[None, :, None, :]
    return jnp.concatenate([x1 * c - x2 * s, x2 * c + x1 * s], axis=-1)


def head_norm_axial_rope(x, g, tabs):
    cr, sr, cc, sc = tabs
    xf = x.astype(jnp.float32)
    xf = xf * lax.rsqrt(jnp.mean(xf * xf, axis=-1, keepdims=True) + EPS) * g.astype(jnp.float32)
    out = jnp.concatenate([_rotate(xf[..., :ROPE_HALF], cr, sr),
                           _rotate(xf[..., ROPE_HALF:], cc, sc)], axis=-1)
    return out.astype(x.dtype)


def block_gqa(q, k, v):
    B, S = q.shape[0], q.shape[1]
    nb = S // Q_BLOCK
    qb = q.reshape(B, nb, Q_BLOCK, N_KV_HEADS, GROUP, HEAD_DIM).transpose(1, 0, 2, 3, 4, 5)
    scale = HEAD_DIM ** -0.5

    def one_block(qblk):
        s = jnp.einsum('bqkgd,bskd->bkgqs', qblk, k, preferred_element_type=jnp.float32) * scale
        p = jax.nn.softmax(s, axis=-1).astype(v.dtype)
        return jnp.einsum('bkgqs,bskd->bqkgd', p, v)

    o = lax.map(one_block, qb)
    return o.transpose(1, 0, 2, 3, 4, 5).reshape(B, S, ATTN_W)


def centred_depthwise_conv(u, w, b):
    S = u.shape[1]
    up = jnp.pad(u, ((0, 0), (CONV_LEFT, CONV_W - 1 - CONV_LEFT), (0, 0)))
    out = b[None, None, :]
    for tap in range(CONV_W):
        out = out + up[:, tap:tap + S, :] * w[tap][None, None, :]
    return out


def _lru_combine(e1, e2):
    a1, b1 = e1
    a2, b2 = e2
    return a1 * a2, a2 * b1 + b2


def rglru_direction(u, w_r, b_r, w_i, b_i, lam, reverse):
    B, S, _ = u.shape
    ub = u.reshape(B, S, N_RNN_BLOCKS, RNN_BLOCK)
    r = jax.nn.sigmoid(jnp.einsum('bsnc,ncd->bsnd', ub, w_r.astype(jnp.float32)).reshape(B, S, D_RNN)
                       + b_r.astype(jnp.float32))
    i = jax.nn.sigmoid(jnp.einsum('bsnc,ncd->bsnd', ub, w_i.astype(jnp.float32)).reshape(B, S, D_RNN)
                       + b_i.astype(jnp.float32))
    log_a = -LRU_C * r * jax.nn.softplus(-lam.astype(jnp.float32))
    a = jnp.exp(log_a)
    bterm = jnp.sqrt(-jnp.expm1(2.0 * log_a)) * (i * u)
    if reverse:
        a = jnp.flip(a, axis=1)
        bterm = jnp.flip(bterm, axis=1)
    _, h = lax.associative_scan(_lru_combine, (a, bterm), axis=1)
    if reverse:
        h = jnp.flip(h, axis=1)
    return h


def setup_inputs(seed: int = 0) -> dict:
    key = jax.random.key(seed)
    ks = jax.random.split(key, 32)
    f32 = jnp.float32

    def nrm(k, shape, fan_in):
        return jax.random.normal(k, shape, f32) * (fan_in ** -0.5)

    def gain(k, shape):
        return 1.0 + 0.02 * jax.random.normal(k, shape, f32)

    u = jax.random.uniform(ks[12], (DEPTH, 2, D_RNN), f32, 0.9, 0.999)
    a0 = u ** (1.0 / LRU_C)
    lam = jnp.log(a0) - jnp.log1p(-a0)
    return {
        'x': jax.random.normal(ks[0], (BATCH, SEQ, D_MODEL), f32),
        'mem': jax.random.normal(ks[1], (BATCH, N_MEM, D_MODEL), f32),
        'mix_norm_g': gain(ks[2], (DEPTH, D_MODEL)),
        'w_in': nrm(ks[3], (DEPTH, D_MODEL, N_IN), D_MODEL),
        'q_norm_g': gain(ks[4], (DEPTH, HEAD_DIM)),
        'k_norm_g': gain(ks[5], (DEPTH, HEAD_DIM)),
        'conv_w': nrm(ks[6], (DEPTH, CONV_W, D_RNN), CONV_W),
        'conv_b': 0.01 * jax.random.normal(ks[7], (DEPTH, D_RNN), f32),
        'lru_w_r': nrm(ks[8], (DEPTH, 2, N_RNN_BLOCKS, RNN_BLOCK, RNN_BLOCK), RNN_BLOCK),
        'lru_b_r': 0.01 * jax.random.normal(ks[9], (DEPTH, 2, D_RNN), f32),
        'lru_w_i': nrm(ks[10], (DEPTH, 2, N_RNN_BLOCKS, RNN_BLOCK, RNN_BLOCK), RNN_BLOCK),
        'lru_b_i': 0.01 * jax.random.normal(ks[11], (DEPTH, 2, D_RNN), f32),
        'lru_lambda': lam,
        'w_attn_branch': nrm(ks[13], (DEPTH, ATTN_W, D_MODEL), ATTN_W),
        'w_rnn_branch': nrm(ks[14], (DEPTH, D_RNN, D_MODEL), D_RNN),
        'w_mix_out': nrm(ks[15], (DEPTH, D_MODEL, D_MODEL), D_MODEL),
        'cross_norm_g': gain(ks[16], (DEPTH, D_MODEL)),
        'mem_norm_g': gain(ks[17], (DEPTH, D_MODEL)),
        'w_xq': nrm(ks[18], (DEPTH, D_MODEL, D_MODEL), D_MODEL),
        'w_xkv': nrm(ks[19], (DEPTH, D_MODEL, 2 * D_MODEL), D_MODEL),
        'w_xo': nrm(ks[20], (DEPTH, D_MODEL, D_MODEL), D_MODEL),
        'mlp_norm_g': gain(ks[21], (DEPTH, D_MODEL)),
        'w_up': nrm(ks[22], (DEPTH, D_MODEL, D_FF), D_MODEL),
        'w_down': nrm(ks[23], (DEPTH, D_FF, D_MODEL), D_FF),
        'final_norm_g': gain(ks[24], (D_MODEL,)),
    }


def reference(x, mem, mix_norm_g, w_in, q_norm_g, k_norm_g, conv_w, conv_b, lru_w_r, lru_b_r,
              lru_w_i, lru_b_i, lru_lambda, w_attn_branch, w_rnn_branch, w_mix_out, cross_norm_g,
              mem_norm_g, w_xq, w_xkv, w_xo, mlp_norm_g, w_up, w_down, final_norm_g):
    B, S, _ = x.shape
    M = mem.shape[1]
    dt = x.dtype
    tabs = axial_rope_tables(S)
    for l in range(DEPTH):
        h = rmsnorm(x, mix_norm_g[l])
        proj = h @ w_in[l]
        q, k, v, u, y, g_a, g_r = jnp.split(proj, SPLITS, axis=-1)
        q = head_norm_axial_rope(q.reshape(B, S, N_Q_HEADS, HEAD_DIM), q_norm_g[l], tabs)
        k = head_norm_axial_rope(k.reshape(B, S, N_KV_HEADS, HEAD_DIM), k_norm_g[l], tabs)
        v = v.reshape(B, S, N_KV_HEADS, HEAD_DIM)
        o_attn = block_gqa(q, k, v)
        uf = centred_depthwise_conv(u.astype(jnp.float32), conv_w[l].astype(jnp.float32),
                                    conv_b[l].astype(jnp.float32))
        h_fwd = rglru_direction(uf, lru_w_r[l, 0], lru_b_r[l, 0], lru_w_i[l, 0], lru_b_i[l, 0],
                                lru_lambda[l, 0], reverse=False)
        h_bwd = rglru_direction(uf, lru_w_r[l, 1], lru_b_r[l, 1], lru_w_i[l, 1], lru_b_i[l, 1],
                                lru_lambda[l, 1], reverse=True)
        o_rnn = ((h_fwd + h_bwd) * jax.nn.gelu(y.astype(jnp.float32))).astype(dt)
        merged = (jax.nn.sigmoid(g_a) * (o_attn @ w_attn_branch[l])
                  + jax.nn.sigmoid(g_r) * (o_rnn @ w_rnn_branch[l]))
        x = x + merged @ w_mix_out[l]
        hc = rmsnorm(x, cross_norm_g[l])
        mn = rmsnorm(mem, mem_norm_g[l])
        xq = (hc @ w_xq[l]).reshape(B, S, N_XHEADS, XHEAD_DIM)
        xk, xv = jnp.split(mn @ w_xkv[l], 2, axis=-1)
        xk = xk.reshape(B, M, N_XHEADS, XHEAD_DIM)
        xv = xv.reshape(B, M, N_XHEADS, XHEAD_DIM)
        s = jnp.einsum('bshd,bmhd->bhsm', xq, xk, preferred_element_type=jnp.float32) * (XHEAD_DIM ** -0.5)
        p = jax.nn.softmax(s, axis=-1).astype(dt)
        xo = jnp.einsum('bhsm,bmhd->bshd', p, xv).reshape(B, S, D_MODEL)
        x = x + xo @ w_xo[l]
        hm = rmsnorm(x, mlp_norm_g[l])
        x = x + jnp.square(jax.nn.relu(hm @ w_up[l])) @ w_down[l]
    return rmsnorm(x, final_norm_g)
```

```python
import numpy as np
import concourse.bass as bass
import concourse.mybir as mybir
from concourse.bass_utils import run_bass_kernel_spmd

F32 = mybir.dt.float32
BF16 = mybir.dt.bfloat16
AF = mybir.ActivationFunctionType
ALU = mybir.AluOpType

EPS = 1e-6
LRU_C = 8.0
ROPE_THETA = 10000.0

FULL_CFG = dict(D=2048, S=4096, NQH=16, NKV=4, M=256, NXH=4, DFF=8192, L=2, GRID_W=64, SPLIT2=True)


class Buf:
    __slots__ = ("name", "writer", "readers", "sem", "apv")

    def __init__(self, name, apv=None, sem=None):
        self.name = name
        self.writer = None
        self.readers = []
        self.sem = sem
        self.apv = apv


class Op:
    __slots__ = ("eng", "fn", "deps", "signal", "ev", "is_dma", "idx")

    def __init__(self, eng, fn):
        self.eng = eng
        self.fn = fn
        self.deps = []
        self.signal = False
        self.ev = None
        self.is_dma = False


ENGS = ("sp", "pe", "act", "dve", "pool")


class Prog:
    def __init__(self, nc, n_dma_sems=48):
        self.nc = nc
        self.ops = {e: [] for e in ENGS}
        self.esem = {}
        self.stack = None
        self.dma_sems = []
        self.dma_cnt = {}
        self.free_dma_sems = []
        self.stage_dma_last = {}
        self.pending_bar = {e: None for e in ENGS}
        self.n_dma_sems = n_dma_sems
        self.stage_bufs = []
        self.nops = 0

    def setup_sems(self, stack):
        for e in ("pe", "act", "dve", "pool"):
            self.esem[e] = stack.enter_context(self.nc.semaphore("s_" + e))
        self.bar_sem = stack.enter_context(self.nc.semaphore("s_bar"))
        self.bar_cnt = 0
        for i in range(self.n_dma_sems):
            s = stack.enter_context(self.nc.semaphore("s_dma%d" % i))
            self.dma_sems.append(s)
            self.dma_cnt[id(s)] = 0
        n_sw = 12
        self.free_sw_sems = list(self.dma_sems[:n_sw])
        self.free_dma_sems = list(self.dma_sems[n_sw:])
        self.sw_ids = set(id(s) for s in self.free_sw_sems)

    def get_dma_sem(self, sw=False):
        return self.free_sw_sems.pop() if sw else self.free_dma_sems.pop()

    def put_dma_sem(self, s):
        (self.free_sw_sems if id(s) in self.sw_ids else self.free_dma_sems).append(s)

    def op(self, eng, fn, r=(), w=()):
        o = Op(eng, fn)
        deps = []
        for b in r:
            if b.writer is not None:
                deps.append(b.writer)
        for b in w:
            if b.writer is not None:
                deps.append(b.writer)
            deps.extend(b.readers)
        if self.pending_bar[eng] is not None:
            deps.append(self.pending_bar[eng])
            self.pending_bar[eng] = None
        seen = set()
        for d in deps:
            if d is o or id(d) in seen:
                continue
            seen.add(id(d))
            if d.eng == "pe" and eng == "pe" and not d.is_dma:
                continue
            d.signal = True
            o.deps.append(d)
        for b in w:
            b.writer = o
            b.readers = []
        for b in r:
            rl = b.readers
            if rl and (not rl[-1].is_dma) and rl[-1].eng == eng:
                rl[-1] = o
            else:
                rl.append(o)
        self.ops[eng].append(o)
        self.nops += 1
        return o

    def dma(self, eng, out_ap, in_ap, r=(), w=(), sbuf=None):
        assert sbuf is not None and sbuf.sem is not None, "dma needs an sbuf Buf with a semaphore"
        assert (id(sbuf.sem) in self.sw_ids) == (eng == "pool"), "semaphore pool / DMA queue mismatch"
        o = self.op(eng, lambda e: e.dma_start(out=out_ap, in_=in_ap), r=r, w=w)
        o.is_dma = True
        o.signal = True
        sid = id(sbuf.sem)
        self.dma_cnt[sid] += 16
        o.ev = (sbuf.sem, self.dma_cnt[sid])
        self.stage_dma_last[sid] = o
        return o

    def barrier(self):
        deps = list(self.stage_dma_last.values())
        for e in ("pe", "act", "dve", "pool"):
            for o_ in reversed(self.ops[e]):
                if not o_.is_dma:
                    deps.append(o_)
                    break
        self.bar_cnt += 1
        cnt = self.bar_cnt
        bs = self.bar_sem
        o = Op("sp", lambda e: e.sem_inc(bs, 1))
        for d in deps:
            d.signal = True
            o.deps.append(d)
        if self.pending_bar["sp"] is not None:
            self.pending_bar["sp"] = None
        o.ev = (bs, cnt)
        o.is_dma = True
        o.signal = False
        self.ops["sp"].append(o)
        for e in ("pe", "act", "dve", "pool"):
            self.pending_bar[e] = o
        self.stage_dma_last = {}
        return o

    def finalize(self):
        for e in ("pe", "act", "dve", "pool"):
            c = 0
            for o in self.ops[e]:
                if o.is_dma:
                    continue
                if o.signal:
                    c += 1
                    o.ev = (self.esem[e], c)

    def emit(self, eng, handle):
        waited = {}
        for o in self.ops[eng]:
            need = {}
            for d in o.deps:
                sem, cnt = d.ev
                k = id(sem)
                if waited.get(k, 0) >= cnt:
                    continue
                if k not in need or need[k][1] < cnt:
                    need[k] = (sem, cnt)
            for k, (sem, cnt) in need.items():
                handle.wait_ge(sem, cnt)
                waited[k] = cnt
            ins = o.fn(handle)
            if o.is_dma:
                if o.ev[0] is not self.bar_sem:
                    ins.then_inc(o.ev[0], 16)
            elif o.signal:
                ins.then_inc(o.ev[0], 1)


class Mem:
    def __init__(self, prog, big_ap, nwords):
        self.P = prog
        self.big = big_ap
        self.nwords = nwords
        self.top = 0
        self.marks = []
        self.stage_sems = []

    def push(self):
        self.marks.append((self.top, len(self.stage_sems)))

    def pop(self):
        self.P.barrier()
        top, ns = self.marks.pop()
        self.top = top
        while len(self.stage_sems) > ns:
            self.P.put_dma_sem(self.stage_sems.pop())

    def alloc(self, name, nelem, dt=F32, dma=False):
        nw = nelem if dt == F32 else (nelem + 1) // 2
        nw = (nw + 7) // 8 * 8
        off = self.top
        self.top += nw
        assert self.top <= self.nwords, "SBUF overflow at %s: %d > %d" % (name, self.top, self.nwords)
        ap = self.big[:, off:off + nw]
        if dt != F32:
            ap = ap.bitcast(dt)
        ap = ap[:, 0:nelem]
        sem = None
        if dma:
            sem = self.P.get_dma_sem(sw=(dma == "sw"))
            self.stage_sems.append(sem)
        return Buf(name, apv=ap, sem=sem)

    def view(self, name, parent, lo, hi, dma=False):
        sem = None
        if dma:
            sem = self.P.get_dma_sem()
            self.stage_sems.append(sem)
        return Buf(name, apv=parent.apv[:, lo:hi], sem=sem)


def rev_ap(ap2d):
    p, f = ap2d.ap[0], ap2d.ap[1]
    n = f[1]
    return bass.AP(ap2d.tensor, ap2d.offset + (n - 1) * f[0], [list(p), [-f[0], n]])


def vec_layout(cfg):
    D, L = cfg["D"], cfg["L"]
    DC = D // 128
    cols = {}
    n = 0

    def add(key, k):
        nonlocal n
        cols[key] = n
        n += k

    for l in range(L):
        for nm in ("mix_norm_g", "cross_norm_g", "mem_norm_g", "mlp_norm_g"):
            add((nm, l), DC)
        add(("q_norm_g", l), 1)
        add(("k_norm_g", l), 1)
        for k in range(4):
            add(("conv_w", l, k), DC)
        add(("conv_b", l), DC)
        for d in range(2):
            add(("lru_b_r", l, d), DC)
            add(("lru_b_i", l, d), DC)
            add(("lru_lambda", l, d), DC)
    add(("final_norm_g",), DC)
    return cols, n


def build_program(cfg):
    D, S, NQH, NKV, M, NXH, DFF, L = (cfg[k] for k in ("D", "S", "NQH", "NKV", "M", "NXH", "DFF", "L"))
    T = S
    H = T // 2
    SPLIT2 = bool(cfg.get("SPLIT2", False))
    TQL = H if SPLIT2 else T
    DC = D // 128
    HD = 128
    AW = NQH * HD
    KVW = NKV * HD
    GROUP = NQH // NKV
    assert AW == D
    XHD = D // NXH
    XDC = XHD // 128
    NIN = AW + 2 * KVW + 2 * D + 2 * D
    o_q, o_k, o_v, o_u, o_y, o_ga, o_gr = 0, AW, AW + KVW, AW + 2 * KVW, AW + 2 * KVW + D, AW + 2 * KVW + 2 * D, AW + 2 * KVW + 3 * D
    TT = min(512, T)
    NTT = T // TT
    MT = min(512, M)
    PT = min(512, T)
    vcols, NV = vec_layout(cfg)

    nc = bass.Bass("TRN2", target_bir_lowering=False)

    def din(name, shape, dt=F32):
        return nc.dram_tensor(name, list(shape), dt, kind="ExternalInput").ap()

    def dscr(name, shape, dt):
        return nc.dram_tensor(name, list(shape), dt, kind="Internal").ap()

    xT_in = din("xT", [D, T])
    memT_in = din("memT", [D, M])
    vecs_in = din("vecs", [128, NV])
    ropeC_in = din("ropeC", [128, S])
    ropeS_in = din("ropeS", [128, S])
    perm_in = din("perm", [128, 128])
    flags_in = din("flags", [128, 2])
    w_in = din("w_in", [L, D, NIN])
    lru_w_r = din("lru_w_r", [L, 2, DC, 128, 128])
    lru_w_i = din("lru_w_i", [L, 2, DC, 128, 128])
    w_ab = din("w_attn_branch", [L, AW, D])
    w_rb = din("w_rnn_branch", [L, D, D])
    w_mo = din("w_mix_out", [L, D, D])
    w_xq = din("w_xq", [L, D, D])
    w_xkv = din("w_xkv", [L, D, 2 * D])
    w_xo = din("w_xo", [L, D, D])
    w_up = din("w_up", [L, D, DFF])
    w_down_t = din("w_down_t", [L, D // 128, 128, DFF])
    outT = nc.dram_tensor("outT", [D, TQL], F32, kind="ExternalOutput").ap()

    xs = [dscr("xs0", [D, T], F32), dscr("xs1", [D, T], F32)]
    xn = dscr("xn", [D, T], BF16)
    qz = dscr("qz", [AW, T], F32)
    kz = dscr("kz", [KVW, T], F32)
    ud = dscr("ud", [D, T], F32)
    gy = dscr("gy", [D, T], F32)
    sga = dscr("sga", [D, T], F32)
    sgr = dscr("sgr", [D, T], F32)
    vtok = dscr("vtok", [T, KVW], BF16)
    qn = dscr("qn", [AW, T], BF16)
    kn = dscr("kn", [KVW, T], BF16)
    oattn = dscr("oattn", [AW, T], BF16)
    ornn = dscr("ornn", [D, T], BF16)
    tmpA = dscr("tmpA", [D, T], F32)
    merged = dscr("merged", [D, T], BF16)
    xq = dscr("xq", [D, T], BF16)
    mn = dscr("mn", [D, M], BF16)
    xk = dscr("xk", [D, M], BF16)
    xvtok = dscr("xvtok", [M, D], BF16)
    xo = dscr("xo", [D, T], BF16)
    hid = dscr("hid", [DFF, T], BF16)

    from contextlib import ExitStack
    stack = ExitStack()
    P = Prog(nc)
    P.setup_sems(stack)
    NW = 46 * 1024
    big_t = stack.enter_context(nc.sbuf_tensor("big", [128, NW], F32))
    mem = Mem(P, big_t[:], NW)
    banks = []
    pairs = []
    for i in range(4):
        pt = stack.enter_context(nc.psum_tensor("pp%d" % i, [128, 1024], F32))
        pairs.append(Buf("pp%d" % i, apv=pt[:]))
        banks.append(Buf("ps%d" % (2 * i), apv=pt[:, 0:512]))
        banks.append(Buf("ps%d" % (2 * i + 1), apv=pt[:, 512:1024]))
    bank_rr = [0]

    def next_bank():
        b = banks[bank_rr[0] % 8]
        bank_rr[0] += 1
        return b

    class BankRot:
        def __init__(self, idxs):
            self.idxs = idxs
            self.i = 0

        def get(self):
            b = banks[self.idxs[self.i % len(self.idxs)]]
            self.i += 1
            return b

    vecs = mem.alloc("vecs", NV, F32, dma=True)
    P.dma("sp", vecs.apv, vecs_in, w=[vecs], sbuf=vecs)
    flags = mem.alloc("flags", 8, F32, dma=True)
    P.dma("sp", flags.apv[:, 0:2], flags_in, w=[flags], sbuf=flags)
    fA = flags.apv[:, 0:1]
    fB = flags.apv[:, 1:2]
    ones = mem.alloc("ones", 128, BF16)
    P.op("dve", lambda e: e.memset(ones.apv, 1.0), w=[ones])
    perm = mem.alloc("perm", 128, BF16, dma="sw")
    P.dma("pool", perm.apv, perm_in, w=[perm], sbuf=perm)
    ncl = L * 2 * DC
    cl = mem.alloc("cl", ncl, F32)
    cl2 = mem.alloc("cl2", ncl, F32)
    cltmp = mem.alloc("cltmp", ncl, F32)

    def clcol(l, d, c):
        return (l * 2 + d) * DC + c

    for l in range(L):
        for d in range(2):
            c0 = vcols[("lru_lambda", l, d)]
            o0 = clcol(l, d, 0)
            src = vecs.apv[:, c0:c0 + DC]
            t_ = cltmp.apv[:, o0:o0 + DC]
            P.op("act", lambda e, s=src, t=t_: e.activation(out=t, in_=s, func=AF.Exp, scale=-1.0), r=[vecs], w=[cltmp])
            P.op("act", lambda e, t=t_: e.activation(out=t, in_=t, func=AF.Ln, bias=1.0), r=[cltmp], w=[cltmp])
            P.op("dve", lambda e, t=t_, o=cl.apv[:, o0:o0 + DC]: e.tensor_scalar(out=o, in0=t, scalar1=-LRU_C, scalar2=None, op0=ALU.mult), r=[cltmp], w=[cl])
            P.op("dve", lambda e, t=t_, o=cl2.apv[:, o0:o0 + DC]: e.tensor_scalar(out=o, in0=t, scalar1=-2.0 * LRU_C, scalar2=None, op0=ALU.mult), r=[cltmp], w=[cl2])

    def vcol(key, c=0):
        k = vcols[key] + c
        return vecs.apv[:, k:k + 1]

    class Rot:
        def __init__(self, name, n, nelem, dt, dma=True):
            self.bufs = [mem.alloc("%s%d" % (name, i), nelem, dt, dma=dma) for i in range(n)]
            self.i = 0

        def get(self):
            b = self.bufs[self.i % len(self.bufs)]
            self.i += 1
            return b

    ew_rr = [0]

    def ew_eng():
        ew_rr[0] += 1
        return "dve" if ew_rr[0] % 2 else "pool"

    def bcast_mid(ap2d, n_mid):
        p, f = ap2d.ap[0], ap2d.ap[1]
        return bass.AP(ap2d.tensor, ap2d.offset, [list(p), [0, n_mid], list(f)])

    def bcast_last(ap2d, n_last):
        p, f = ap2d.ap[0], ap2d.ap[1]
        return bass.AP(ap2d.tensor, ap2d.offset, [list(p), list(f), [0, n_last]])

    def prep_norm(src, gkey, dst, Tn, tts, final=False):
        mem.push()
        ntt = Tn // tts
        xt_rot = Rot("pn_x", 2 if final else 3, DC * tts, F32)
        sq_rot = Rot("pn_sq", 2, DC * tts, BF16, dma=False)
        out_rot = Rot("pn_o", 2, DC * tts, F32 if final else BF16)
        rs_rot = Rot("pn_rs", 2, tts, F32, dma=False)
        srcv = src.rearrange("(c p) t -> p c t", p=128)
        dstv = dst.rearrange("(c p) t -> p c t", p=128)
        g0 = vcols[gkey]
        g_b = bcast_last(vecs.apv[:, g0:g0 + DC], tts)
        for tt in range(ntt):
            xt = xt_rot.get()
            t0 = tt * tts
            x3 = xt.apv.rearrange("p (c t) -> p c t", c=DC)
            P.dma("sp", x3, srcv[:, :, t0:t0 + tts], w=[xt], sbuf=xt)
            sq = sq_rot.get()
            P.op("act", lambda e, o=sq.apv, i=xt.apv: e.activation(out=o, in_=i, func=AF.Square), r=[xt], w=[sq])
            bk = next_bank()
            for c in range(DC):
                P.op("pe", lambda e, o=bk.apv[:, 0:tts], rh=sq.apv[:, c * tts:(c + 1) * tts], st=(c == 0), sp=(c == DC - 1):
                     e.matmul(o, ones.apv, rh, start=st, stop=sp), r=[sq, ones], w=[bk])
            P.op("dve", lambda e, o=x3, g=g_b: e.tensor_tensor(out=o, in0=o, in1=g, op=ALU.mult), r=[xt, vecs, sq], w=[xt])
            rs = rs_rot.get()
            P.op("act", lambda e, o=rs.apv, i=bk.apv[:, 0:tts]: e.activation(out=o, in_=i, func=AF.Ln, scale=1.0 / D, bias=EPS_T.apv[:, 0:1]), r=[bk, EPS_T], w=[rs])
            P.op("act", lambda e, o=rs.apv: e.activation(out=o, in_=o, func=AF.Exp, scale=-0.5), r=[rs], w=[rs])
            ob = out_rot.get()
            o3 = ob.apv.rearrange("p (c t) -> p c t", c=DC)
            P.op("dve", lambda e, o=o3, i=x3, r_=bcast_mid(rs.apv, DC): e.tensor_tensor(out=o, in0=i, in1=r_, op=ALU.mult), r=[xt, rs], w=[ob])
            P.dma("sp", dstv[:, :, t0:t0 + tts], o3, r=[ob], sbuf=ob)
        mem.pop()

    def linear(src, K, W, colranges, Tn, tts, epi, epi_setup=None, Wtiled=None, epi_pre=None):
        mem.push()
        KC = K // 128
        big_k = KC > 16
        TS = min(Tn, max(tts, ((128 if big_k else 64) * 1024) // (KC * 2)))
        NGW = max(128, 8192 // KC) if Wtiled is None else 128
        NKG = 4 if KC >= 4 else 1
        kpg = KC // NKG
        inb = [mem.alloc("lin_in%d" % i, kpg * TS, BF16, dma=True) for i in range(NKG)]
        wrot = Rot("lin_w", 2 if big_k else 3, KC * NGW, BF16, dma="sw")
        ctx = epi_setup() if epi_setup else None
        srcv = src.rearrange("(c p) t -> p c t", p=128)
        Wv = W.rearrange("(c p) n -> p c n", p=128) if W is not None else None
        groups = []
        for (c0, c1) in colranges:
            n = c0
            while n < c1:
                w_ = min(NGW, c1 - n)
                groups.append((n, w_))
                n += w_
        nts = TS // tts
        setsz = min(4, nts)
        for ts in range(Tn // TS):
            for i in range(NKG):
                P.dma("sp", inb[i].apv.rearrange("p (c t) -> p c t", c=kpg),
                      srcv[:, i * kpg:(i + 1) * kpg, ts * TS:(ts + 1) * TS], w=[inb[i]], sbuf=inb[i])
            for (n0, gw) in groups:
                wb = wrot.get()
                w3 = wb.apv[:, 0:KC * gw].rearrange("p (c n) -> p c n", c=KC)
                if Wtiled is not None:
                    assert gw == 128
                    P.dma("pool", wb.apv[:, 0:KC * 128], Wtiled[n0 // 128], w=[wb], sbuf=wb)
                else:
                    P.dma("pool", w3, Wv[:, :, n0:n0 + gw], w=[wb], sbuf=wb)
                for nci in range(gw // 128):
                    for s0 in range(0, nts, setsz):
                        bks = [next_bank() for _ in range(setsz)]
                        pres = [epi_pre(n0 + nci * 128, ts * TS + (s0 + j) * tts, ctx) if epi_pre else None for j in range(setsz)]
                        for kc in range(KC):
                            ib = inb[kc // kpg]
                            kl = kc % kpg
                            for j in range(setsz):
                                tl = (s0 + j) * tts
                                P.op("pe", lambda e, o=bks[j].apv[:, 0:tts], lh=w3[:, kc, nci * 128:(nci + 1) * 128],
                                     rh=ib.apv[:, kl * TS + tl: kl * TS + tl + tts], st=(kc == 0), sp=(kc == KC - 1):
                                     e.matmul(o, lh, rh, start=st, stop=sp), r=[wb, ib], w=[bks[j]])
                        for j in range(setsz):
                            epi(n0 + nci * 128, ts * TS + (s0 + j) * tts, bks[j], ctx, pres[j])
        mem.pop()

    def linear_tm(src, K, W, c0, ncols, Tn, dst):
        mem.push()
        KC = K // 128
        TS = min(Tn, (64 * 1024) // (KC * 2))
        NKG = 4 if KC >= 4 else 1
        kpg = KC // NKG
        inb = [mem.alloc("ltm_in%d" % i, kpg * TS, BF16, dma=True) for i in range(NKG)]
        GW = min(512, ncols)
        wrot = Rot("ltm_w", 2, KC * GW, BF16, dma="sw")
        orot = Rot("ltm_o", 3, GW, BF16)
        srcv = src.rearrange("(c p) t -> p c t", p=128)
        Wv = W.rearrange("(c p) n -> p c n", p=128)
        for ts in range(Tn // TS):
            for i in range(NKG):
                P.dma("sp", inb[i].apv.rearrange("p (c t) -> p c t", c=kpg),
                      srcv[:, i * kpg:(i + 1) * kpg, ts * TS:(ts + 1) * TS], w=[inb[i]], sbuf=inb[i])
            for g0 in range(0, ncols, GW):
                wb = wrot.get()
                w3 = wb.apv.rearrange("p (c n) -> p c n", c=KC)
                P.dma("pool", w3, Wv[:, :, c0 + g0:c0 + g0 + GW], w=[wb], sbuf=wb)
                for st_ in range(TS // 128):
                    bk = next_bank()
                    for kc in range(KC):
                        ib = inb[kc // kpg]
                        kl = kc % kpg
                        P.op("pe", lambda e, o=bk.apv[:, 0:GW], lh=ib.apv[:, kl * TS + st_ * 128: kl * TS + st_ * 128 + 128],
                             rh=w3[:, kc, :], st=(kc == 0), sp=(kc == KC - 1):
                             e.matmul(o, lh, rh, start=st, stop=sp), r=[wb, ib], w=[bk])
                    ob = orot.get()
                    eng = "act" if st_ % 2 else "dve"
                    if eng == "act":
                        P.op("act", lambda e, o=ob.apv, i=bk.apv[:, 0:GW]: e.activation(out=o, in_=i, func=AF.Copy), r=[bk], w=[ob])
                    else:
                        P.op("dve", lambda e, o=ob.apv, i=bk.apv[:, 0:GW]: e.tensor_copy(out=o, in_=i), r=[bk], w=[ob])
                    tok0 = ts * TS + st_ * 128
                    P.dma("sp", dst[tok0:tok0 + 128, g0:g0 + GW], ob.apv, r=[ob], sbuf=ob)
        mem.pop()

    def epi_store(route, tts):
        def setup():
            return dict(f=Rot("ep_f", 4, tts, F32), b=Rot("ep_b", 4, tts, BF16), k=[0])

        def epi(col0, t0, bk, ctx, pre=None):
            dst, kind = route(col0)
            ctx["k"][0] += 1
            ps = bk.apv[:, 0:tts]
            if kind == "f32":
                ob = ctx["f"].get()
                if ctx["k"][0] % 2:
                    P.op("act", lambda e, o=ob.apv, i=ps: e.activation(out=o, in_=i, func=AF.Copy), r=[bk], w=[ob])
                else:
                    P.op("dve", lambda e, o=ob.apv, i=ps: e.tensor_copy(out=o, in_=i), r=[bk], w=[ob])
            elif kind == "bf16":
                ob = ctx["b"].get()
                if ctx["k"][0] % 2:
                    P.op("act", lambda e, o=ob.apv, i=ps: e.activation(out=o, in_=i, func=AF.Copy), r=[bk], w=[ob])
                else:
                    P.op("dve", lambda e, o=ob.apv, i=ps: e.tensor_copy(out=o, in_=i), r=[bk], w=[ob])
            elif kind == "gelu":
                ob = ctx["f"].get()
                P.op("act", lambda e, o=ob.apv, i=ps: e.activation(out=o, in_=i, func=AF.Gelu), r=[bk], w=[ob])
            elif kind == "sigmoid":
                ob = ctx["f"].get()
                P.op("act", lambda e, o=ob.apv, i=ps: e.activation(out=o, in_=i, func=AF.Sigmoid), r=[bk], w=[ob])
            elif kind == "relu2":
                tb = ctx["f"].get()
                ob = ctx["b"].get()
                P.op("act", lambda e, o=tb.apv, i=ps: e.activation(out=o, in_=i, func=AF.Relu), r=[bk], w=[tb])
                P.op("dve", lambda e, o=ob.apv, i=tb.apv: e.tensor_tensor(out=o, in0=i, in1=i, op=ALU.mult), r=[tb], w=[ob])
            P.dma("sp", dst[:, t0:t0 + tts], ob.apv, r=[ob], sbuf=ob)
        return epi, setup

    def epi_gate(gate_src, dst, tts, addsrc=None, out_dt=F32):
        def setup():
            return dict(g=Rot("eg_g", 8, tts, F32), a=Rot("eg_a", 8, tts, F32) if addsrc is not None else None,
                        o=Rot("eg_o", 3, tts, out_dt))

        def pre(col0, t0, ctx):
            gb = ctx["g"].get()
            P.dma("sp", gb.apv, gate_src[col0:col0 + 128, t0:t0 + tts], w=[gb], sbuf=gb)
            ab = None
            if addsrc is not None:
                ab = ctx["a"].get()
                P.dma("sp", ab.apv, addsrc[col0:col0 + 128, t0:t0 + tts], w=[ab], sbuf=ab)
            return gb, ab

        def epi(col0, t0, bk, ctx, pre_):
            gb, ab = pre_
            ps = bk.apv[:, 0:tts]
            ob = ctx["o"].get()
            if addsrc is None:
                P.op("dve", lambda e, o=ob.apv, i=ps, g=gb.apv: e.tensor_tensor(out=o, in0=i, in1=g, op=ALU.mult), r=[bk, gb], w=[ob])
            else:
                P.op("dve", lambda e, o=gb.apv, i=ps, g=gb.apv: e.tensor_tensor(out=o, in0=i, in1=g, op=ALU.mult), r=[bk, gb], w=[gb])
                P.op("dve", lambda e, o=ob.apv, i=gb.apv, a_=ab.apv: e.tensor_tensor(out=o, in0=i, in1=a_, op=ALU.add), r=[gb, ab], w=[ob])
            P.dma("sp", dst[col0:col0 + 128, t0:t0 + tts], ob.apv, r=[ob], sbuf=ob)
        return epi, setup, pre

    def epi_resid(xold, xnew, tts):
        def setup():
            return dict(x=Rot("er_x", 8, tts, F32))

        def pre(col0, t0, ctx):
            xb = ctx["x"].get()
            P.dma("sp", xb.apv, xold[col0:col0 + 128, t0:t0 + tts], w=[xb], sbuf=xb)
            return xb

        def epi(col0, t0, bk, ctx, xb):
            P.op("dve", lambda e, o=xb.apv, i=bk.apv[:, 0:tts]: e.tensor_tensor(out=o, in0=i, in1=o, op=ALU.add), r=[bk, xb], w=[xb])
            P.dma("sp", xnew[col0:col0 + 128, t0:t0 + tts], xb.apv, r=[xb], sbuf=xb)
        return epi, setup, pre

    def qk_post(l, nq_tiles):
        mem.push()
        NB = 4
        z_rot = Rot("qk_z", 2 * NB, TT, F32)
        sq_rot = Rot("qk_sq", 2 * NB, TT, BF16, dma=False)
        zg_rot = Rot("qk_zg", 2 * NB, TT, F32, dma=False)
        zb_rot = Rot("qk_zb", 2 * NB, TT, BF16, dma=False)
        hr_rot = Rot("qk_hr", 2 * NB, TT, F32, dma=False)
        t1_rot = Rot("qk_t1", 2 * NB, TT, F32, dma=False)
        t2_rot = Rot("qk_t2", 2 * NB, TT, F32, dma=False)
        o_rot = Rot("qk_o", 2 * NB, TT, BF16)
        c_rot = Rot("qk_c", 2, TT, F32)
        s_rot = Rot("qk_s", 2, TT, F32)
        items_q = [(qz, qn, h, ("q_norm_g", l)) for h in range(NQH)]
        items_k = [(kz, kn, h, ("k_norm_g", l)) for h in range(NKV)]
        for tt in range(NTT):
            items = (items_q if tt < nq_tiles else []) + items_k
            t0 = tt * TT
            cb = c_rot.get()
            sb = s_rot.get()
            P.dma("sp", cb.apv, ropeC_in[:, t0:t0 + TT], w=[cb], sbuf=cb)
            P.dma("sp", sb.apv, ropeS_in[:, t0:t0 + TT], w=[sb], sbuf=sb)
            for b0 in range(0, len(items), NB):
                batch = items[b0:b0 + NB]
                n = len(batch)
                zs = [z_rot.get() for _ in range(n)]
                for i, (srcz, dstn, h, gkey) in enumerate(batch):
                    P.dma("sp", zs[i].apv, srcz[h * 128:(h + 1) * 128, t0:t0 + TT], w=[zs[i]], sbuf=zs[i])
                sqs = [sq_rot.get() for _ in range(n)]
                for i in range(n):
                    P.op("act", lambda e, o=sqs[i].apv, i_=zs[i].apv: e.activation(out=o, in_=i_, func=AF.Square), r=[zs[i]], w=[sqs[i]])
                for i in range(n):
                    P.op("pe", lambda e, o=banks[i].apv[:, 0:TT], rh=sqs[i].apv: e.matmul(o, ones.apv, rh, start=True, stop=True), r=[sqs[i], ones], w=[banks[i]])
                zbs = [zb_rot.get() for _ in range(n)]
                for i, (srcz, dstn, h, gkey) in enumerate(batch):
                    P.op("act", lambda e, o=zbs[i].apv, i_=zs[i].apv, g=vcol(gkey): e.activation(out=o, in_=i_, func=AF.Identity, scale=g), r=[zs[i], vecs], w=[zbs[i]])
                for i in range(n):
                    P.op("pe", lambda e, o=banks[4 + i].apv[:, 0:TT], rh=zbs[i].apv: e.matmul(o, perm.apv, rh, start=True, stop=True), r=[zbs[i], perm], w=[banks[4 + i]])
                zgs = [zg_rot.get() for _ in range(n)]
                for i, (srcz, dstn, h, gkey) in enumerate(batch):
                    P.op("act", lambda e, o=zgs[i].apv, i_=zs[i].apv, g=vcol(gkey): e.activation(out=o, in_=i_, func=AF.Identity, scale=g), r=[zs[i], vecs], w=[zgs[i]])
                hrs = [hr_rot.get() for _ in range(n)]
                for i in range(n):
                    P.op("act", lambda e, o=hrs[i].apv, i_=banks[i].apv[:, 0:TT]: e.activation(out=o, in_=i_, func=AF.Ln, scale=1.0 / 128, bias=EPS_T.apv[:, 0:1]), r=[banks[i], EPS_T], w=[hrs[i]])
                for i in range(n):
                    P.op("act", lambda e, o=hrs[i].apv: e.activation(out=o, in_=o, func=AF.Exp, scale=-0.5), r=[hrs[i]], w=[hrs[i]])
                t1s = [t1_rot.get() for _ in range(n)]
                t2s = [t2_rot.get() for _ in range(n)]
                for i in range(n):
                    P.op("pool", lambda e, o=t1s[i].apv, i_=zgs[i].apv, c=cb.apv: e.tensor_tensor(out=o, in0=i_, in1=c, op=ALU.mult), r=[zgs[i], cb], w=[t1s[i]])
                for i in range(n):
                    P.op("dve", lambda e, o=t2s[i].apv, i_=banks[4 + i].apv[:, 0:TT], s_=sb.apv: e.tensor_tensor(out=o, in0=i_, in1=s_, op=ALU.mult), r=[banks[4 + i], sb], w=[t2s[i]])
                for i in range(n):
                    P.op("dve", lambda e, o=t2s[i].apv, a_=t1s[i].apv, b_=t2s[i].apv: e.tensor_tensor(out=o, in0=a_, in1=b_, op=ALU.add), r=[t1s[i], t2s[i]], w=[t2s[i]])
                for i, (srcz, dstn, h, gkey) in enumerate(batch):
                    ob = o_rot.get()
                    P.op("dve", lambda e, o=ob.apv, a_=t2s[i].apv, b_=hrs[i].apv: e.tensor_tensor(out=o, in0=a_, in1=b_, op=ALU.mult), r=[t2s[i], hrs[i]], w=[ob])
                    P.dma("sp", dstn[h * 128:(h + 1) * 128, t0:t0 + TT], ob.apv, r=[ob], sbuf=ob)
        mem.pop()

    def attention(n_qtiles):
        mem.push()
        SC = S // 128
        assert SC % 2 == 0
        NP = SC // 2
        scale = float(HD) ** -0.5
        kT = [mem.alloc("at_k%d" % h, S, BF16, dma=True) for h in range(NKV)]
        for h in range(NKV):
            P.dma("sp", kT[h].apv, kn[h * 128:(h + 1) * 128, :], w=[kT[h]], sbuf=kT[h])
        NVG = 4 if SC >= 4 else 1
        spg = SC // NVG
        vb = [mem.alloc("at_v%d" % i, spg * KVW, BF16, dma=True) for i in range(NVG)]
        vv = vtok.rearrange("(c p) n -> p c n", p=128)
        for i in range(NVG):
            P.dma("sp", vb[i].apv.rearrange("p (c n) -> p c n", c=spg), vv[:, i * spg:(i + 1) * spg, :], w=[vb[i]], sbuf=vb[i])
        q_rot = Rot("at_q", 2, T, BF16)
        p_rot = Rot("at_p", 4, 2 * TT, BF16, dma=False)
        rd_rot = Rot("at_rd", 2, TT, F32, dma=False)
        o_rot = Rot("at_o", 2, TT, BF16)
        accd_rot = Rot("at_ad", 2, 2 * TT, F32, dma=False)
        accp_rot = Rot("at_ap", 2, 2 * TT, F32, dma=False)
        hl_rot = Rot("at_hl", 2, 2 * TT, BF16, dma=False)
        bro = BankRot([0, 1])
        brd = BankRot([2, 3])
        spair = [pairs[2], pairs[3]]
        spi = [0]
        POOL_EVERY = 5
        for hq in range(NQH):
            hk = hq // GROUP
            qb = q_rot.get()
            P.dma("sp", qb.apv[:, 0:n_qtiles * TT], qn[hq * 128:(hq + 1) * 128, 0:n_qtiles * TT], w=[qb], sbuf=qb)
            for qt in range(n_qtiles):
                qs = qb.apv[:, qt * TT:(qt + 1) * TT]
                bo = bro.get()
                bd = brd.get()
                accd = accd_rot.get()
                accp = accp_rot.get()
                first = {"dve": True, "pool": True}
                sp_ = [None] * NP

                def score(jp):
                    pr = spair[spi[0] % 2]
                    spi[0] += 1
                    sp_[jp] = pr
                    for hh in range(2):
                        j = 2 * jp + hh
                        P.op("pe", lambda e, o=pr.apv[:, hh * 512: hh * 512 + TT], lh=kT[hk].apv[:, j * 128:(j + 1) * 128], rh=qs:
                             e.matmul(o, lh, rh, start=True, stop=True), r=[kT[hk], qb], w=[pr])
                score(0)
                for jp in range(NP):
                    if jp + 1 < NP:
                        score(jp + 1)
                    pb = p_rot.get()
                    pr = sp_[jp]
                    if TT == 512:
                        P.op("act", lambda e, o=pb.apv, i=pr.apv: e.activation(out=o, in_=i, func=AF.Exp, scale=scale), r=[pr], w=[pb])
                    else:
                        for hh in range(2):
                            P.op("act", lambda e, o=pb.apv[:, hh * TT:(hh + 1) * TT], i=pr.apv[:, hh * 512: hh * 512 + TT]:
                                 e.activation(out=o, in_=i, func=AF.Exp, scale=scale), r=[pr], w=[pb])
                    for hh in range(2):
                        j = 2 * jp + hh
                        vbuf = vb[j // spg]
                        vl = (j % spg) * KVW + hk * 128
                        ph = pb.apv[:, hh * TT:(hh + 1) * TT]
                        P.op("pe", lambda e, o=bo.apv[:, 0:TT], lh=vbuf.apv[:, vl:vl + 128], rh=ph, st=(j == 0), sp=(j == SC - 1):
                             e.matmul(o, lh, rh, start=st, stop=sp), r=[vbuf, pb], w=[bo])
                    eng = "pool" if (jp % POOL_EVERY == POOL_EVERY - 1 and NP >= POOL_EVERY) else "dve"
                    acc = accp if eng == "pool" else accd
                    if first[eng]:
                        first[eng] = False
                        P.op(eng, lambda e, o=acc.apv, i=pb.apv: e.tensor_copy(out=o, in_=i), r=[pb], w=[acc])
                    else:
                        P.op(eng, lambda e, o=acc.apv, i=pb.apv: e.tensor_tensor(out=o, in0=o, in1=i, op=ALU.add), r=[pb, acc], w=[acc])
                if not first["pool"]:
                    P.op("dve", lambda e, o=accd.apv, i=accp.apv: e.tensor_tensor(out=o, in0=o, in1=i, op=ALU.add), r=[accd, accp], w=[accd])
                P.op("dve", lambda e, o=accd.apv[:, 0:TT], i=accd.apv[:, TT:2 * TT]: e.tensor_tensor(out=o, in0=o, in1=i, op=ALU.add), r=[accd], w=[accd])
                hl = hl_rot.get()
                P.op("dve", lambda e, o=hl.apv[:, 0:TT], i=accd.apv[:, 0:TT]: e.tensor_copy(out=o, in_=i), r=[accd], w=[hl])
                P.op("dve", lambda e, o=hl.apv[:, TT:2 * TT], a_=accd.apv[:, 0:TT], h_=hl.apv[:, 0:TT]: e.tensor_tensor(out=o, in0=a_, in1=h_, op=ALU.subtract), r=[accd, hl], w=[hl])
                P.op("pe", lambda e, o=bd.apv[:, 0:TT], rh=hl.apv[:, 0:TT]: e.matmul(o, ones.apv, rh, start=True, stop=False), r=[ones, hl], w=[bd])
                P.op("pe", lambda e, o=bd.apv[:, 0:TT], rh=hl.apv[:, TT:2 * TT]: e.matmul(o, ones.apv, rh, start=False, stop=True), r=[ones, hl], w=[bd])
                rd = rd_rot.get()
                P.op("act", lambda e, o=rd.apv, i=bd.apv[:, 0:TT]: e.activation(out=o, in_=i, func=AF.Ln), r=[bd], w=[rd])
                P.op("act", lambda e, o=rd.apv: e.activation(out=o, in_=o, func=AF.Exp, scale=-1.0), r=[rd], w=[rd])
                ob = o_rot.get()
                P.op("dve", lambda e, o=ob.apv, i=bo.apv[:, 0:TT], r_=rd.apv: e.tensor_tensor(out=o, in0=i, in1=r_, op=ALU.mult), r=[bo, rd], w=[ob])
                P.dma("sp", oattn[hq * 128:(hq + 1) * 128, qt * TT:(qt + 1) * TT], ob.apv, r=[ob], sbuf=ob)
        mem.pop()

    def lru(l, out_T):
        mem.push()
        GB = min(4, NTT)
        GW_ = GB * TT
        upA = mem.alloc("lr_uA", H + 3, F32, dma=True)
        upB = mem.alloc("lr_uB", H + 3, F32, dma=True)
        ufs = [mem.alloc("lr_uf%d" % i, T, F32) for i in range(2)]
        ubs = [mem.alloc("lr_ub%d" % i, T, BF16, dma=True) for i in range(2)]
        ufh = [[mem.view("lr_uf%d%d" % (i, h), ufs[i], h * H, (h + 1) * H) for h in range(2)] for i in range(2)]
        ubh = [[mem.view("lr_ub%d%d" % (i, h), ubs[i], h * H, (h + 1) * H) for h in range(2)] for i in range(2)]
        afull = mem.alloc("lr_a", T, F32)
        bfull = mem.alloc("lr_b", T, F32)
        hf = mem.alloc("lr_hf", T, F32)
        hb = mem.alloc("lr_hb", T, F32)
        gyb = mem.alloc("lr_gy", out_T, F32, dma=True)
        ini = mem.alloc("lr_ini", 8, F32)
        nbat = NTT // GB
        a_t = [mem.view("lr_a%d" % i, afull, i * GW_, (i + 1) * GW_) for i in range(nbat)]
        b_t = [mem.view("lr_b%d" % i, bfull, i * GW_, (i + 1) * GW_) for i in range(nbat)]
        wg = [[mem.alloc("lr_w%d%d" % (d, g), 128, BF16, dma="sw") for g in range(2)] for d in range(2)]
        r_rot = Rot("lr_r", 2, GW_, F32, dma=False)
        i_rot = Rot("lr_i", 1, GW_, F32, dma=False)
        e_rot = Rot("lr_e", 1, GW_, F32, dma=False)
        npair = max(1, GB // 2)

        def tsmul(o, i, f, rbufs, wbufs):
            P.op("dve", lambda e, o=o, i=i, f=f: e.tensor_scalar(out=o, in0=i, scalar1=f, scalar2=None, op0=ALU.mult), r=rbufs + [flags], w=wbufs)

        def scan(o, a_, b_, init, rbufs, wbufs, rev=False):
            if rev:
                o, a_, b_ = rev_ap(o), rev_ap(a_), rev_ap(b_)
            P.op("dve", lambda e, o=o, a_=a_, b_=b_, init=init: e.tensor_tensor_scan(out=o, data0=a_, data1=b_, initial=init, op0=ALU.mult, op1=ALU.add),
                 r=rbufs, w=wbufs)

        def load_u(c):
            P.dma("sp", upA.apv[:, 2:H + 2], ud[c * 128:(c + 1) * 128, 0:H], w=[upA], sbuf=upA)
            P.dma("sp", upB.apv[:, 2:H + 2], ud[c * 128:(c + 1) * 128, H:T], w=[upB], sbuf=upB)

        def conv_act1(c, s_):
            tsmul(upA.apv[:, 0:2], upB.apv[:, H:H + 2], fB, [upB], [upA])
            tsmul(upA.apv[:, H + 2:H + 3], upB.apv[:, 2:3], fA, [upB], [upA])
            tsmul(upB.apv[:, 0:2], upA.apv[:, H:H + 2], fA, [upA], [upB])
            tsmul(upB.apv[:, H + 2:H + 3], upA.apv[:, 2:3], fB, [upA], [upB])
            for h_, up_ in enumerate((upA, upB)):
                P.op("act", lambda e, o=ufh[s_][h_].apv, i=up_.apv[:, 0:H], w0=vcol(("conv_w", l, 0), c), b=vcol(("conv_b", l), c):
                     e.activation(out=o, in_=i, func=AF.Identity, scale=w0, bias=b), r=[up_, vecs], w=[ufh[s_][h_]])

        def conv_dve(c, s_):
            for h_, up_ in enumerate((upA, upB)):
                for k in range(1, 4):
                    P.op("dve", lambda e, o=ufh[s_][h_].apv, i=up_.apv[:, k:k + H], wk=vcol(("conv_w", l, k), c):
                         e.scalar_tensor_tensor(out=o, in0=i, scalar=wk, in1=o, op0=ALU.mult, op1=ALU.add), r=[up_, ufh[s_][h_], vecs], w=[ufh[s_][h_]])

        def conv_act2(c, s_):
            for h_ in range(2):
                P.op("act", lambda e, o=ubh[s_][h_].apv, i=ufh[s_][h_].apv: e.activation(out=o, in_=i, func=AF.Copy), r=[ufh[s_][h_]], w=[ubh[s_][h_]])

        def gates(c, d, s_):
            k = clcol(l, d, c)
            uf_, ub_ = ufs[s_], ubs[s_]
            for bi in range(nbat):
                c0 = bi * GW_
                for ti in range(GB):
                    pr_r = pairs[ti // 2]
                    pr_i = pairs[2 + ti // 2]
                    hh = ti % 2
                    rh = ub_.apv[:, c0 + ti * TT: c0 + (ti + 1) * TT]
                    P.op("pe", lambda e, o=pr_r.apv[:, hh * 512: hh * 512 + TT], lh=wg[d][0].apv, rh=rh: e.matmul(o, lh, rh, start=True, stop=True), r=[wg[d][0]] + ubh[s_], w=[pr_r])
                    P.op("pe", lambda e, o=pr_i.apv[:, hh * 512: hh * 512 + TT], lh=wg[d][1].apv, rh=rh: e.matmul(o, lh, rh, start=True, stop=True), r=[wg[d][1]] + ubh[s_], w=[pr_i])
                rb = r_rot.get()
                ib_ = i_rot.get()
                eb = e_rot.get()
                for (dst_, pbase, bkey) in ((rb, 0, ("lru_b_r", l, d)), (ib_, 2, ("lru_b_i", l, d))):
                    if TT == 512 and GB >= 2:
                        for pi in range(npair):
                            P.op("act", lambda e, o=dst_.apv[:, pi * 1024:(pi + 1) * 1024], i=pairs[pbase + pi].apv, b=vcol(bkey, c):
                                 e.activation(out=o, in_=i, func=AF.Sigmoid, bias=b), r=[pairs[pbase + pi], vecs], w=[dst_])
                    else:
                        for ti in range(GB):
                            P.op("act", lambda e, o=dst_.apv[:, ti * TT:(ti + 1) * TT], i=pairs[pbase + ti // 2].apv[:, (ti % 2) * 512:(ti % 2) * 512 + TT], b=vcol(bkey, c):
                                 e.activation(out=o, in_=i, func=AF.Sigmoid, bias=b), r=[pairs[pbase + ti // 2], vecs], w=[dst_])
                P.op("act", lambda e, o=a_t[bi].apv, i=rb.apv, sc_=cl.apv[:, k:k + 1]: e.activation(out=o, in_=i, func=AF.Exp, scale=sc_), r=[rb, cl], w=[a_t[bi]])
                P.op("act", lambda e, o=eb.apv, i=rb.apv, sc_=cl2.apv[:, k:k + 1]: e.activation(out=o, in_=i, func=AF.Exp, scale=sc_), r=[rb, cl2], w=[eb])
                P.op("act", lambda e, o=eb.apv: e.activation(out=o, in_=o, func=AF.Sqrt, scale=-1.0, bias=ONE_T.apv[:, 0:1]), r=[eb, ONE_T], w=[eb])
                P.op("dve", lambda e, o=ib_.apv, u_=uf_.apv[:, c0:c0 + GW_]: e.tensor_tensor(out=o, in0=o, in1=u_, op=ALU.mult), r=[ib_] + ufh[s_], w=[ib_])
                P.op("dve", lambda e, o=b_t[bi].apv, i=ib_.apv, s2=eb.apv: e.tensor_tensor(out=o, in0=i, in1=s2, op=ALU.mult), r=[ib_, eb], w=[b_t[bi]])

        def scans(d):
            ab_ = a_t + b_t
            aA, aB = afull.apv[:, 0:H], afull.apv[:, H:T]
            bA, bB = bfull.apv[:, 0:H], bfull.apv[:, H:T]
            if d == 0:
                scan(hf.apv[:, H:T], aB, bB, 0.0, ab_, [hf])
                tsmul(ini.apv[:, 0:1], hf.apv[:, T - 1:T], fB, [hf], [ini])
                scan(hf.apv[:, 0:H], aA, bA, ini.apv[:, 0:1], ab_ + [ini], [hf])
                if out_T > H:
                    tsmul(ini.apv[:, 1:2], hf.apv[:, H - 1:H], fA, [hf], [ini])
                    scan(hf.apv[:, H:T], aB, bB, ini.apv[:, 1:2], ab_ + [ini], [hf])
            else:
                scan(hb.apv[:, 0:H], aA, bA, 0.0, ab_, [hb], rev=True)
                tsmul(ini.apv[:, 2:3], hb.apv[:, 0:1], fB, [hb], [ini])
                scan(hb.apv[:, H:T], aB, bB, ini.apv[:, 2:3], ab_ + [ini], [hb], rev=True)
                tsmul(ini.apv[:, 3:4], hb.apv[:, H:H + 1], fA, [hb], [ini])
                scan(hb.apv[:, 0:H], aA, bA, ini.apv[:, 3:4], ab_ + [ini], [hb], rev=True)

        load_u(0)
        conv_act1(0, 0)
        conv_dve(0, 0)
        conv_act2(0, 0)
        for c in range(DC):
            s_ = c % 2
            n_ = 1 - s_
            more = c + 1 < DC
            P.dma("sp", gyb.apv, gy[c * 128:(c + 1) * 128, 0:out_T], w=[gyb], sbuf=gyb)
            for d in range(2):
                P.dma("pool", wg[d][0].apv, lru_w_r[l, d, c], w=[wg[d][0]], sbuf=wg[d][0])
                P.dma("pool", wg[d][1].apv, lru_w_i[l, d, c], w=[wg[d][1]], sbuf=wg[d][1])
            if more:
                load_u(c + 1)
            gates(c, 0, s_)
            if more:
                conv_act1(c + 1, n_)
            scans(0)
            if more:
                conv_dve(c + 1, n_)
            gates(c, 1, s_)
            if more:
                conv_act2(c + 1, n_)
            scans(1)
            P.op("dve", lambda e, o=hf.apv[:, 0:out_T], b=hb.apv[:, 0:out_T]: e.tensor_tensor(out=o, in0=o, in1=b, op=ALU.add), r=[hf, hb], w=[hf])
            ob = ubs[s_]
            P.op("dve", lambda e, o=ob.apv[:, 0:out_T], a=hf.apv[:, 0:out_T], g=gyb.apv: e.tensor_tensor(out=o, in0=a, in1=g, op=ALU.mult), r=[hf, gyb] + ubh[s_], w=[ob] + ubh[s_])
            P.dma("sp", ornn[c * 128:(c + 1) * 128, 0:out_T], ob.apv[:, 0:out_T], r=[ob] + ubh[s_], sbuf=ob)
        mem.pop()

    def cross_attention(n_qtiles):
        mem.push()
        MC = M // 128
        scale = float(XHD) ** -0.5
        kb = mem.alloc("ca_k", DC * M, BF16, dma=True)
        P.dma("sp", kb.apv.rearrange("p (c m) -> p c m", c=DC), xk.rearrange("(c p) m -> p c m", p=128), w=[kb], sbuf=kb)
        vb_ = mem.alloc("ca_v", MC * D, BF16, dma=True)
        P.dma("sp", vb_.apv.rearrange("p (c n) -> p c n", c=MC), xvtok.rearrange("(c p) n -> p c n", p=128), w=[vb_], sbuf=vb_)
        q_rot = Rot("ca_q", 2, DC * TT, BF16)
        p_rot = Rot("ca_p", 2 * MC, TT, BF16, dma=False)
        rd_rot = Rot("ca_rd", 2, TT, F32, dma=False)
        o_rot = Rot("ca_o", 3, TT, BF16)
        xqv = xq.rearrange("(c p) t -> p c t", p=128)
        brs = BankRot([0, 1, 2, 3])
        brd = BankRot([4, 5])
        bro = BankRot([6, 7])
        for qt in range(n_qtiles):
            t0 = qt * TT
            qb = q_rot.get()
            P.dma("sp", qb.apv.rearrange("p (c t) -> p c t", c=DC), xqv[:, :, t0:t0 + TT], w=[qb], sbuf=qb)
            for h in range(NXH):
                pbs = []
                for mc in range(MC):
                    bs = brs.get()
                    for dc in range(XDC):
                        ch = h * XDC + dc
                        P.op("pe", lambda e, o=bs.apv[:, 0:TT], lh=kb.apv[:, ch * M + mc * 128: ch * M + mc * 128 + 128],
                             rh=qb.apv[:, ch * TT:(ch + 1) * TT], st=(dc == 0), sp=(dc == XDC - 1):
                             e.matmul(o, lh, rh, start=st, stop=sp), r=[kb, qb], w=[bs])
                    pb = p_rot.get()
                    P.op("act", lambda e, o=pb.apv, i=bs.apv[:, 0:TT]: e.activation(out=o, in_=i, func=AF.Exp, scale=scale), r=[bs], w=[pb])
                    pbs.append(pb)
                bd = brd.get()
                for mc in range(MC):
                    P.op("pe", lambda e, o=bd.apv[:, 0:TT], rh=pbs[mc].apv, st=(mc == 0), sp=(mc == MC - 1):
                         e.matmul(o, ones.apv, rh, start=st, stop=sp), r=[ones, pbs[mc]], w=[bd])
                rd = rd_rot.get()
                P.op("dve", lambda e, o=rd.apv, i=bd.apv[:, 0:TT]: e.reciprocal(out=o, in_=i), r=[bd], w=[rd])
                for dc in range(XDC):
                    ch = h * XDC + dc
                    bo = bro.get()
                    for mc in range(MC):
                        P.op("pe", lambda e, o=bo.apv[:, 0:TT], lh=vb_.apv[:, mc * D + ch * 128: mc * D + ch * 128 + 128], rh=pbs[mc].apv,
                             st=(mc == 0), sp=(mc == MC - 1): e.matmul(o, lh, rh, start=st, stop=sp), r=[vb_, pbs[mc]], w=[bo])
                    ob = o_rot.get()
                    P.op("dve", lambda e, o=ob.apv, i=bo.apv[:, 0:TT], r_=rd.apv: e.tensor_tensor(out=o, in0=i, in1=r_, op=ALU.mult), r=[bo, rd], w=[ob])
                    P.dma("sp", xo[ch * 128:(ch + 1) * 128, t0:t0 + TT], ob.apv, r=[ob], sbuf=ob)
        mem.pop()

    EPS_T = mem.alloc("eps_t", 8, F32)
    ONE_T = mem.alloc("one_t", 8, F32)
    P.op("dve", lambda e: e.memset(EPS_T.apv, EPS), w=[EPS_T])
    P.op("dve", lambda e: e.memset(ONE_T.apv, 1.0), w=[ONE_T])
    P.barrier()

    xcur = xT_in
    xi = 0
    for l in range(L):
        Tq = TQL if l == L - 1 else T
        NQT = Tq // TT
        prep_norm(xcur, ("mix_norm_g", l), xn, T, PT)

        def route_in(col0):
            if col0 < o_k:
                return qz[col0 - o_q: col0 - o_q + 128], "f32"
            if col0 < o_v:
                return kz[col0 - o_k: col0 - o_k + 128], "f32"
            if col0 < o_y:
                return ud[col0 - o_u: col0 - o_u + 128], "f32"
            if col0 < o_ga:
                return gy[col0 - o_y: col0 - o_y + 128], "gelu"
            if col0 < o_gr:
                return sga[col0 - o_ga: col0 - o_ga + 128], "sigmoid"
            return sgr[col0 - o_gr: col0 - o_gr + 128], "sigmoid"
        ep, su = epi_store(route_in, TT)
        if Tq == T:
            linear(xn, D, w_in[l], [(0, o_v), (o_u, NIN)], T, TT, ep, su)
        else:
            linear(xn, D, w_in[l], [(o_k, o_v), (o_u, o_y)], T, TT, ep, su)
            linear(xn, D, w_in[l], [(0, o_k), (o_y, NIN)], Tq, TT, ep, su)
        linear_tm(xn, D, w_in[l], o_v, KVW, T, vtok)
        qk_post(l, NQT)
        attention(NQT)
        lru(l, Tq)
        ep, su, pr = epi_gate(sga, tmpA, TT)
        linear(oattn, AW, w_ab[l], [(0, D)], Tq, TT, ep, su, epi_pre=pr)
        ep, su, pr = epi_gate(sgr, merged, TT, addsrc=tmpA, out_dt=BF16)
        linear(ornn, D, w_rb[l], [(0, D)], Tq, TT, ep, su, epi_pre=pr)
        xnext = xs[xi % 2]; xi += 1
        ep, su, pr = epi_resid(xcur, xnext, TT)
        linear(merged, D, w_mo[l], [(0, D)], Tq, TT, ep, su, epi_pre=pr)
        xcur = xnext
        prep_norm(xcur, ("cross_norm_g", l), xn, Tq, PT)
        ep, su = epi_store(lambda c0: (xq[c0:c0 + 128], "bf16"), TT)
        linear(xn, D, w_xq[l], [(0, D)], Tq, TT, ep, su)
        prep_norm(memT_in, ("mem_norm_g", l), mn, M, min(MT, PT))
        ep, su = epi_store(lambda c0: (xk[c0:c0 + 128], "bf16"), MT)
        linear(mn, D, w_xkv[l], [(0, D)], M, MT, ep, su)
        linear_tm(mn, D, w_xkv[l], D, D, M, xvtok)
        cross_attention(NQT)
        xnext = xs[xi % 2]; xi += 1
        ep, su, pr = epi_resid(xcur, xnext, TT)
        linear(xo, D, w_xo[l], [(0, D)], Tq, TT, ep, su, epi_pre=pr)
        xcur = xnext
        prep_norm(xcur, ("mlp_norm_g", l), xn, Tq, PT)
        ep, su = epi_store(lambda c0: (hid[c0:c0 + 128], "relu2"), TT)
        linear(xn, D, w_up[l], [(0, DFF)], Tq, TT, ep, su)
        xnext = xs[xi % 2]; xi += 1
        ep, su, pr = epi_resid(xcur, xnext, TT)
        linear(hid, DFF, None, [(0, D)], Tq, TT, ep, su, Wtiled=w_down_t[l], epi_pre=pr)
        xcur = xnext
    prep_norm(xcur, ("final_norm_g",), outT, TQL, PT, final=True)
    P.barrier()

    P.finalize()
    with nc.Block() as block:
        @block.sync
        def _(e):
            P.emit("sp", e)

        @block.tensor
        def _(e):
            P.emit("pe", e)

        @block.scalar
        def _(e):
            P.emit("act", e)

        @block.vector
        def _(e):
            P.emit("dve", e)

        @block.gpsimd
        def _(e):
            P.emit("pool", e)
    stack.close()
    return nc, P


def rope_tables(cfg):
    S, GW = cfg["S"], cfg["GRID_W"]
    rows_n = S // GW
    row = np.repeat(np.arange(rows_n, dtype=np.float32), GW)
    col = np.tile(np.arange(GW, dtype=np.float32), rows_n)
    n_freq = 32
    inv = (np.float32(ROPE_THETA) ** (-np.arange(n_freq, dtype=np.float32) / np.float32(n_freq))).astype(np.float32)
    ang_r = (row[:, None] * inv[None, :]).astype(np.float32)
    ang_c = (col[:, None] * inv[None, :]).astype(np.float32)
    cr, sr, cc, sc = np.cos(ang_r), np.sin(ang_r), np.cos(ang_c), np.sin(ang_c)
    C = np.concatenate([cr, cr, cc, cc], axis=1).T
    Sg = np.concatenate([-sr, sr, -sc, sc], axis=1).T
    return np.ascontiguousarray(C, dtype=np.float32), np.ascontiguousarray(Sg, dtype=np.float32)


def perm_matrix():
    Pm = np.zeros((128, 128), np.float32)
    for blk in range(2):
        b = blk * 64
        for i in range(32):
            Pm[b + i, b + 32 + i] = 1.0
            Pm[b + 32 + i, b + i] = 1.0
    return Pm


def pack_vecs(cfg, inp):
    cols, NV = vec_layout(cfg)
    L, D = cfg["L"], cfg["D"]
    DC = D // 128
    V = np.zeros((128, NV), np.float32)

    def put(key, vec):
        v = np.asarray(vec, np.float32)
        k = v.size // 128
        V[:, cols[key]:cols[key] + k] = v.reshape(k, 128).T

    for l in range(L):
        for nm in ("mix_norm_g", "cross_norm_g", "mem_norm_g", "mlp_norm_g"):
            put((nm, l), inp[nm][l])
        put(("q_norm_g", l), inp["q_norm_g"][l])
        put(("k_norm_g", l), inp["k_norm_g"][l])
        for k in range(4):
            put(("conv_w", l, k), inp["conv_w"][l, k])
        put(("conv_b", l), inp["conv_b"][l])
        for d in range(2):
            put(("lru_b_r", l, d), inp["lru_b_r"][l, d])
            put(("lru_b_i", l, d), inp["lru_b_i"][l, d])
            put(("lru_lambda", l, d), inp["lru_lambda"][l, d])
    put(("final_norm_g",), inp["final_norm_g"])
    return V


def make_in_maps(cfg, inp, cores):
    C, Sg = rope_tables(cfg)
    Pm = perm_matrix()
    V = pack_vecs(cfg, inp)
    S = cfg["S"]
    Hh = S // 2
    shared = {"vecs": V, "perm": Pm}
    for k in ("w_in", "lru_w_r", "lru_w_i", "w_attn_branch", "w_rnn_branch", "w_mix_out", "w_xq", "w_xkv", "w_xo", "w_up"):
        shared[k] = np.ascontiguousarray(np.asarray(inp[k], np.float32))
    wd = np.asarray(inp["w_down"], np.float32)
    L_, DFF_, D_ = wd.shape
    shared["w_down_t"] = np.ascontiguousarray(
        wd.reshape(L_, DFF_ // 128, 128, D_ // 128, 128).transpose(0, 3, 2, 1, 4).reshape(L_, D_ // 128, 128, DFF_))
    maps = []
    for (b, hf) in cores:
        idx = np.concatenate([np.arange(hf * Hh, hf * Hh + Hh), np.arange((1 - hf) * Hh, (1 - hf) * Hh + Hh)])
        m = dict(shared)
        m["xT"] = np.ascontiguousarray(np.asarray(inp["x"][b], np.float32)[idx].T)
        m["memT"] = np.ascontiguousarray(np.asarray(inp["mem"][b], np.float32).T)
        m["ropeC"] = np.ascontiguousarray(C[:, idx])
        m["ropeS"] = np.ascontiguousarray(Sg[:, idx])
        fl = np.zeros((128, 2), np.float32)
        fl[:, 0] = 1.0 - hf
        fl[:, 1] = float(hf)
        m["flags"] = fl
        maps.append(m)
    return maps


_NC_CACHE = {}


def kernel(**inputs):
    cfg = FULL_CFG
    B = inputs["x"].shape[0]
    S = cfg["S"]
    if "nc" not in _NC_CACHE:
        _NC_CACHE["nc"] = build_program(cfg)[0]
    nc = _NC_CACHE["nc"]
    if cfg.get("SPLIT2"):
        cores = [(b, hf) for b in range(B) for hf in range(2)]
    else:
        cores = [(b, 0) for b in range(B)]
    in_maps = make_in_maps(cfg, inputs, cores)
    res = run_bass_kernel_spmd(nc, in_maps, core_ids=list(range(len(cores))))
    out = np.zeros((B, S, cfg["D"]), np.float32)
    Hh = S // 2
    for (b, hf), r in zip(cores, res.results):
        o = np.ascontiguousarray(r["outT"].T)
        if cfg.get("SPLIT2"):
            out[b, hf * Hh:(hf + 1) * Hh] = o
        else:
            out[b] = o
    return out
```

```python
import numpy as np
import concourse.bass as bass
import concourse.mybir as mybir
from concourse.bass_utils import run_bass_kernel_spmd

F32 = mybir.dt.float32
BF16 = mybir.dt.bfloat16
AF = mybir.ActivationFunctionType
ALU = mybir.AluOpType

EPS = 1e-6
LRU_C = 8.0
ROPE_THETA = 10000.0

FULL_CFG = dict(D=2048, S=4096, NQH=16, NKV=4, M=256, NXH=4, DFF=8192, L=2, GRID_W=64, SPLIT2=True)


class Buf:
    __slots__ = ("name", "writer", "readers", "sem", "apv")

    def __init__(self, name, apv=None, sem=None):
        self.name = name
        self.writer = None
        self.readers = []
        self.sem = sem
        self.apv = apv


class Op:
    __slots__ = ("eng", "fn", "deps", "signal", "ev", "is_dma", "idx")

    def __init__(self, eng, fn):
        self.eng = eng
        self.fn = fn
        self.deps = []
        self.signal = False
        self.ev = None
        self.is_dma = False


ENGS = ("sp", "pe", "act", "dve", "pool")


class Prog:
    def __init__(self, nc, n_dma_sems=48):
        self.nc = nc
        self.ops = {e: [] for e in ENGS}
        self.esem = {}
        self.stack = None
        self.dma_sems = []
        self.dma_cnt = {}
        self.free_dma_sems = []
        self.stage_dma_last = {}
        self.pending_bar = {e: None for e in ENGS}
        self.n_dma_sems = n_dma_sems
        self.stage_bufs = []
        self.nops = 0

    def setup_sems(self, stack):
        for e in ("pe", "act", "dve", "pool"):
            self.esem[e] = stack.enter_context(self.nc.semaphore("s_" + e))
        self.bar_sem = stack.enter_context(self.nc.semaphore("s_bar"))
        self.bar_cnt = 0
        for i in range(self.n_dma_sems):
            s = stack.enter_context(self.nc.semaphore("s_dma%d" % i))
            self.dma_sems.append(s)
            self.dma_cnt[id(s)] = 0
        n_sw = 12
        self.free_sw_sems = list(self.dma_sems[:n_sw])
        self.free_dma_sems = list(self.dma_sems[n_sw:])
        self.sw_ids = set(id(s) for s in self.free_sw_sems)

    def get_dma_sem(self, sw=False):
        return self.free_sw_sems.pop() if sw else self.free_dma_sems.pop()

    def put_dma_sem(self, s):
        (self.free_sw_sems if id(s) in self.sw_ids else self.free_dma_sems).append(s)

    def op(self, eng, fn, r=(), w=()):
        o = Op(eng, fn)
        deps = []
        for b in r:
            if b.writer is not None:
                deps.append(b.writer)
        for b in w:
            if b.writer is not None:
                deps.append(b.writer)
            deps.extend(b.readers)
        if self.pending_bar[eng] is not None:
            deps.append(self.pending_bar[eng])
            self.pending_bar[eng] = None
        seen = set()
        for d in deps:
            if d is o or id(d) in seen:
                continue
            seen.add(id(d))
            if d.eng == "pe" and eng == "pe" and not d.is_dma:
                continue
            d.signal = True
            o.deps.append(d)
        for b in w:
            b.writer = o
            b.readers = []
        for b in r:
            rl = b.readers
            if rl and (not rl[-1].is_dma) and rl[-1].eng == eng:
                rl[-1] = o
            else:
                rl.append(o)
        self.ops[eng].append(o)
        self.nops += 1
        return o

    def dma(self, eng, out_ap, in_ap, r=(), w=(), sbuf=None):
        assert sbuf is not None and sbuf.sem is not None, "dma needs an sbuf Buf with a semaphore"
        assert (id(sbuf.sem) in self.sw_ids) == (eng == "pool"), "semaphore pool / DMA queue mismatch"
        o = self.op(eng, lambda e: e.dma_start(out=out_ap, in_=in_ap), r=r, w=w)
        o.is_dma = True
        o.signal = True
        sid = id(sbuf.sem)
        self.dma_cnt[sid] += 16
        o.ev = (sbuf.sem, self.dma_cnt[sid])
        self.stage_dma_last[sid] = o
        return o

    def barrier(self):
        deps = list(self.stage_dma_last.values())
        for e in ("pe", "act", "dve", "pool"):
            for o_ in reversed(self.ops[e]):
                if not o_.is_dma:
                    deps.append(o_)
                    break
        self.bar_cnt += 1
        cnt = self.bar_cnt
        bs = self.bar_sem
        o = Op("sp", lambda e: e.sem_inc(bs, 1))
        for d in deps:
            d.signal = True
            o.deps.append(d)
        if self.pending_bar["sp"] is not None:
            self.pending_bar["sp"] = None
        o.ev = (bs, cnt)
        o.is_dma = True
        o.signal = False
        self.ops["sp"].append(o)
        for e in ("pe", "act", "dve", "pool"):
            self.pending_bar[e] = o
        self.stage_dma_last = {}
        return o

    def finalize(self):
        for e in ("pe", "act", "dve", "pool"):
            c = 0
            for o in self.ops[e]:
                if o.is_dma:
                    continue
                if o.signal:
                    c += 1
                    o.ev = (self.esem[e], c)

    def emit(self, eng, handle):
        waited = {}
        for o in self.ops[eng]:
            need = {}
            for d in o.deps:
                sem, cnt = d.ev
                k = id(sem)
                if waited.get(k, 0) >= cnt:
                    continue
                if k not in need or need[k][1] < cnt:
                    need[k] = (sem, cnt)
            for k, (sem, cnt) in need.items():
                handle.wait_ge(sem, cnt)
                waited[k] = cnt
            ins = o.fn(handle)
            if o.is_dma:
                if o.ev[0] is not self.bar_sem:
                    ins.then_inc(o.ev[0], 16)
            elif o.signal:
                ins.then_inc(o.ev[0], 1)


class Mem:
    def __init__(self, prog, big_ap, nwords):
        self.P = prog
        self.big = big_ap
        self.nwords = nwords
        self.top = 0
        self.marks = []
        self.stage_sems = []

    def push(self):
        self.marks.append((self.top, len(self.stage_sems)))

    def pop(self):
        self.P.barrier()
        top, ns = self.marks.pop()
        self.top = top
        while len(self.stage_sems) > ns:
            self.P.put_dma_sem(self.stage_sems.pop())

    def alloc(self, name, nelem, dt=F32, dma=False):
        nw = nelem if dt == F32 else (nelem + 1) // 2
        nw = (nw + 7) // 8 * 8
        off = self.top
        self.top += nw
        assert self.top <= self.nwords, "SBUF overflow at %s: %d > %d" % (name, self.top, self.nwords)
        ap = self.big[:, off:off + nw]
        if dt != F32:
            ap = ap.bitcast(dt)
        ap = ap[:, 0:nelem]
        sem = None
        if dma:
            sem = self.P.get_dma_sem(sw=(dma == "sw"))
            self.stage_sems.append(sem)
        return Buf(name, apv=ap, sem=sem)

    def view(self, name, parent, lo, hi, dma=False):
        sem = None
        if dma:
            sem = self.P.get_dma_sem()
            self.stage_sems.append(sem)
        return Buf(name, apv=parent.apv[:, lo:hi], sem=sem)


def rev_ap(ap2d):
    p, f = ap2d.ap[0], ap2d.ap[1]
    n = f[1]
    return bass.AP(ap2d.tensor, ap2d.offset + (n - 1) * f[0], [list(p), [-f[0], n]])


def vec_layout(cfg):
    D, L = cfg["D"], cfg["L"]
    DC = D // 128
    cols = {}
    n = 0

    def add(key, k):
        nonlocal n
        cols[key] = n
        n += k

    for l in range(L):
        for nm in ("mix_norm_g", "cross_norm_g", "mem_norm_g", "mlp_norm_g"):
            add((nm, l), DC)
        add(("q_norm_g", l), 1)
        add(("k_norm_g", l), 1)
        for k in range(4):
            add(("conv_w", l, k), DC)
        add(("conv_b", l), DC)
        for d in range(2):
            add(("lru_b_r", l, d), DC)
            add(("lru_b_i", l, d), DC)
            add(("lru_lambda", l, d), DC)
    add(("final_norm_g",), DC)
    return cols, n


def build_program(cfg):
    D, S, NQH, NKV, M, NXH, DFF, L = (cfg[k] for k in ("D", "S", "NQH", "NKV", "M", "NXH", "DFF", "L"))
    T = S
    H = T // 2
    SPLIT2 = bool(cfg.get("SPLIT2", False))
    TQL = H if SPLIT2 else T
    DC = D // 128
    HD = 128
    AW = NQH * HD
    KVW = NKV * HD
    GROUP = NQH // NKV
    assert AW == D
    XHD = D // NXH
    XDC = XHD // 128
    NIN = AW + 2 * KVW + 2 * D + 2 * D
    o_q, o_k, o_v, o_u, o_y, o_ga, o_gr = 0, AW, AW + KVW, AW + 2 * KVW, AW + 2 * KVW + D, AW + 2 * KVW + 2 * D, AW + 2 * KVW + 3 * D
    TT = min(512, T)
    NTT = T // TT
    MT = min(512, M)
    PT = min(512, T)
    vcols, NV = vec_layout(cfg)

    nc = bass.Bass("TRN2", target_bir_lowering=False)

    def din(name, shape, dt=F32):
        return nc.dram_tensor(name, list(shape), dt, kind="ExternalInput").ap()

    def dscr(name, shape, dt):
        return nc.dram_tensor(name, list(shape), dt, kind="Internal").ap()

    xT_in = din("xT", [D, T])
    memT_in = din("memT", [D, M])
    vecs_in = din("vecs", [128, NV])
    ropeC_in = din("ropeC", [128, S])
    ropeS_in = din("ropeS", [128, S])
    perm_in = din("perm", [128, 128])
    flags_in = din("flags", [128, 2])
    w_in = din("w_in", [L, D, NIN])
    lru_w_r = din("lru_w_r", [L, 2, DC, 128, 128])
    lru_w_i = din("lru_w_i", [L, 2, DC, 128, 128])
    w_ab = din("w_attn_branch", [L, AW, D])
    w_rb = din("w_rnn_branch", [L, D, D])
    w_mo = din("w_mix_out", [L, D, D])
    w_xq = din("w_xq", [L, D, D])
    w_xkv = din("w_xkv", [L, D, 2 * D])
    w_xo = din("w_xo", [L, D, D])
    w_up = din("w_up", [L, D, DFF])
    w_down_t = din("w_down_t", [L, D // 128, 128, DFF])
    outT = nc.dram_tensor("outT", [D, TQL], F32, kind="ExternalOutput").ap()

    xs = [dscr("xs0", [D, T], F32), dscr("xs1", [D, T], F32)]
    xn = dscr("xn", [D, T], BF16)
    qz = dscr("qz", [AW, T], F32)
    kz = dscr("kz", [KVW, T], F32)
    ud = dscr("ud", [D, T], F32)
    gy = dscr("gy", [D, T], F32)
    sga = dscr("sga", [D, T], F32)
    sgr = dscr("sgr", [D, T], F32)
    vtok = dscr("vtok", [T, KVW], BF16)
    qn = dscr("qn", [AW, T], BF16)
    kn = dscr("kn", [KVW, T], BF16)
    oattn = dscr("oattn", [AW, T], BF16)
    ornn = dscr("ornn", [D, T], BF16)
    tmpA = dscr("tmpA", [D, T], F32)
    merged = dscr("merged", [D, T], BF16)
    xq = dscr("xq", [D, T], BF16)
    mn = dscr("mn", [D, M], BF16)
    xk = dscr("xk", [D, M], BF16)
    xvtok = dscr("xvtok", [M, D], BF16)
    xo = dscr("xo", [D, T], BF16)
    hid = dscr("hid", [DFF, T], BF16)

    from contextlib import ExitStack
    stack = ExitStack()
    P = Prog(nc)
    P.setup_sems(stack)
    NW = 46 * 1024
    big_t = stack.enter_context(nc.sbuf_tensor("big", [128, NW], F32))
    mem = Mem(P, big_t[:], NW)
    banks = []
    pairs = []
    for i in range(4):
        pt = stack.enter_context(nc.psum_tensor("pp%d" % i, [128, 1024], F32))
        pairs.append(Buf("pp%d" % i, apv=pt[:]))
        banks.append(Buf("ps%d" % (2 * i), apv=pt[:, 0:512]))
        banks.append(Buf("ps%d" % (2 * i + 1), apv=pt[:, 512:1024]))
    bank_rr = [0]

    def next_bank():
        b = banks[bank_rr[0] % 8]
        bank_rr[0] += 1
        return b

    class BankRot:
        def __init__(self, idxs):
            self.idxs = idxs
            self.i = 0

        def get(self):
            b = banks[self.idxs[self.i % len(self.idxs)]]
            self.i += 1
            return b

    vecs = mem.alloc("vecs", NV, F32, dma=True)
    P.dma("sp", vecs.apv, vecs_in, w=[vecs], sbuf=vecs)
    flags = mem.alloc("flags", 8, F32, dma=True)
    P.dma("sp", flags.apv[:, 0:2], flags_in, w=[flags], sbuf=flags)
    fA = flags.apv[:, 0:1]
    fB = flags.apv[:, 1:2]
    ones = mem.alloc("ones", 128, BF16)
    P.op("dve", lambda e: e.memset(ones.apv, 1.0), w=[ones])
    perm = mem.alloc("perm", 128, BF16, dma="sw")
    P.dma("pool", perm.apv, perm_in, w=[perm], sbuf=perm)
    ncl = L * 2 * DC
    cl = mem.alloc("cl", ncl, F32)
    cl2 = mem.alloc("cl2", ncl, F32)
    cltmp = mem.alloc("cltmp", ncl, F32)

    def clcol(l, d, c):
        return (l * 2 + d) * DC + c

    for l in range(L):
        for d in range(2):
            c0 = vcols[("lru_lambda", l, d)]
            o0 = clcol(l, d, 0)
            src = vecs.apv[:, c0:c0 + DC]
            t_ = cltmp.apv[:, o0:o0 + DC]
            P.op("act", lambda e, s=src, t=t_: e.activation(out=t, in_=s, func=AF.Exp, scale=-1.0), r=[vecs], w=[cltmp])
            P.op("act", lambda e, t=t_: e.activation(out=t, in_=t, func=AF.Ln, bias=1.0), r=[cltmp], w=[cltmp])
            P.op("dve", lambda e, t=t_, o=cl.apv[:, o0:o0 + DC]: e.tensor_scalar(out=o, in0=t, scalar1=-LRU_C, scalar2=None, op0=ALU.mult), r=[cltmp], w=[cl])
            P.op("dve", lambda e, t=t_, o=cl2.apv[:, o0:o0 + DC]: e.tensor_scalar(out=o, in0=t, scalar1=-2.0 * LRU_C, scalar2=None, op0=ALU.mult), r=[cltmp], w=[cl2])

    def vcol(key, c=0):
        k = vcols[key] + c
        return vecs.apv[:, k:k + 1]

    class Rot:
        def __init__(self, name, n, nelem, dt, dma=True):
            self.bufs = [mem.alloc("%s%d" % (name, i), nelem, dt, dma=dma) for i in range(n)]
            self.i = 0

        def get(self):
            b = self.bufs[self.i % len(self.bufs)]
            self.i += 1
            return b

    ew_rr = [0]

    def ew_eng():
        ew_rr[0] += 1
        return "dve" if ew_rr[0] % 2 else "pool"

    def bcast_mid(ap2d, n_mid):
        p, f = ap2d.ap[0], ap2d.ap[1]
        return bass.AP(ap2d.tensor, ap2d.offset, [list(p), [0, n_mid], list(f)])

    def bcast_last(ap2d, n_last):
        p, f = ap2d.ap[0], ap2d.ap[1]
        return bass.AP(ap2d.tensor, ap2d.offset, [list(p), list(f), [0, n_last]])

    def prep_norm(src, gkey, dst, Tn, tts, final=False):
        mem.push()
        ntt = Tn // tts
        xt_rot = Rot("pn_x", 2 if final else 3, DC * tts, F32)
        sq_rot = Rot("pn_sq", 2, DC * tts, BF16, dma=False)
        out_rot = Rot("pn_o", 2, DC * tts, F32 if final else BF16)
        rs_rot = Rot("pn_rs", 2, tts, F32, dma=False)
        srcv = src.rearrange("(c p) t -> p c t", p=128)
        dstv = dst.rearrange("(c p) t -> p c t", p=128)
        g0 = vcols[gkey]
        g_b = bcast_last(vecs.apv[:, g0:g0 + DC], tts)
        depth = len(xt_rot.bufs)
        xts = {}

        def load(tt_):
            if tt_ < ntt:
                xb_ = xt_rot.get()
                P.dma("sp", xb_.apv.rearrange("p (c t) -> p c t", c=DC), srcv[:, :, tt_ * tts:(tt_ + 1) * tts], w=[xb_], sbuf=xb_)
                xts[tt_] = xb_
        for i_ in range(depth - 1):
            load(i_)
        for tt in range(ntt):
            load(tt + depth - 1)
            xt = xts.pop(tt)
            t0 = tt * tts
            x3 = xt.apv.rearrange("p (c t) -> p c t", c=DC)
            sq = sq_rot.get()
            P.op("act", lambda e, o=sq.apv, i=xt.apv: e.activation(out=o, in_=i, func=AF.Square), r=[xt], w=[sq])
            bk = next_bank()
            for c in range(DC):
                P.op("pe", lambda e, o=bk.apv[:, 0:tts], rh=sq.apv[:, c * tts:(c + 1) * tts], st=(c == 0), sp=(c == DC - 1):
                     e.matmul(o, ones.apv, rh, start=st, stop=sp), r=[sq, ones], w=[bk])
            P.op("dve", lambda e, o=x3, g=g_b: e.tensor_tensor(out=o, in0=o, in1=g, op=ALU.mult), r=[xt, vecs, sq], w=[xt])
            rs = rs_rot.get()
            P.op("act", lambda e, o=rs.apv, i=bk.apv[:, 0:tts]: e.activation(out=o, in_=i, func=AF.Ln, scale=1.0 / D, bias=EPS_T.apv[:, 0:1]), r=[bk, EPS_T], w=[rs])
            P.op("act", lambda e, o=rs.apv: e.activation(out=o, in_=o, func=AF.Exp, scale=-0.5), r=[rs], w=[rs])
            ob = out_rot.get()
            o3 = ob.apv.rearrange("p (c t) -> p c t", c=DC)
            P.op("dve", lambda e, o=o3, i=x3, r_=bcast_mid(rs.apv, DC): e.tensor_tensor(out=o, in0=i, in1=r_, op=ALU.mult), r=[xt, rs], w=[ob])
            P.dma("sp", dstv[:, :, t0:t0 + tts], o3, r=[ob], sbuf=ob)
        mem.pop()

    def linear(src, K, W, colranges, Tn, tts, epi, epi_setup=None, Wtiled=None, epi_pre=None):
        mem.push()
        KC = K // 128
        big_k = KC > 16
        TS = min(Tn, max(tts, ((128 if big_k else 64) * 1024) // (KC * 2)))
        NGW = max(128, 8192 // KC) if Wtiled is None else 128
        NKG = 4 if KC >= 4 else 1
        kpg = KC // NKG
        inb = [mem.alloc("lin_in%d" % i, kpg * TS, BF16, dma=True) for i in range(NKG)]
        wrot = Rot("lin_w", 2 if big_k else 3, KC * NGW, BF16, dma="sw")
        ctx = epi_setup() if epi_setup else None
        srcv = src.rearrange("(c p) t -> p c t", p=128)
        Wv = W.rearrange("(c p) n -> p c n", p=128) if W is not None else None
        groups = []
        for (c0, c1) in colranges:
            n = c0
            while n < c1:
                w_ = min(NGW, c1 - n)
                groups.append((n, w_))
                n += w_
        nts = TS // tts
        setsz = min(4, nts)
        for ts in range(Tn // TS):
            for i in range(NKG):
                P.dma("sp", inb[i].apv.rearrange("p (c t) -> p c t", c=kpg),
                      srcv[:, i * kpg:(i + 1) * kpg, ts * TS:(ts + 1) * TS], w=[inb[i]], sbuf=inb[i])
            for (n0, gw) in groups:
                wb = wrot.get()
                w3 = wb.apv[:, 0:KC * gw].rearrange("p (c n) -> p c n", c=KC)
                if Wtiled is not None:
                    assert gw == 128
                    P.dma("pool", wb.apv[:, 0:KC * 128], Wtiled[n0 // 128], w=[wb], sbuf=wb)
                else:
                    P.dma("pool", w3, Wv[:, :, n0:n0 + gw], w=[wb], sbuf=wb)
                for nci in range(gw // 128):
                    for s0 in range(0, nts, setsz):
                        bks = [next_bank() for _ in range(setsz)]
                        pres = [epi_pre(n0 + nci * 128, ts * TS + (s0 + j) * tts, ctx) if epi_pre else None for j in range(setsz)]
                        for kc in range(KC):
                            ib = inb[kc // kpg]
                            kl = kc % kpg
                            for j in range(setsz):
                                tl = (s0 + j) * tts
                                P.op("pe", lambda e, o=bks[j].apv[:, 0:tts], lh=w3[:, kc, nci * 128:(nci + 1) * 128],
                                     rh=ib.apv[:, kl * TS + tl: kl * TS + tl + tts], st=(kc == 0), sp=(kc == KC - 1):
                                     e.matmul(o, lh, rh, start=st, stop=sp), r=[wb, ib], w=[bks[j]])
                        for j in range(setsz):
                            epi(n0 + nci * 128, ts * TS + (s0 + j) * tts, bks[j], ctx, pres[j])
        mem.pop()

    def linear_tm(src, K, W, c0, ncols, Tn, dst):
        mem.push()
        KC = K // 128
        TS = min(Tn, (64 * 1024) // (KC * 2))
        NKG = 4 if KC >= 4 else 1
        kpg = KC // NKG
        inb = [mem.alloc("ltm_in%d" % i, kpg * TS, BF16, dma=True) for i in range(NKG)]
        GW = min(512, ncols)
        wrot = Rot("ltm_w", 2, KC * GW, BF16, dma="sw")
        orot = Rot("ltm_o", 3, GW, BF16)
        srcv = src.rearrange("(c p) t -> p c t", p=128)
        Wv = W.rearrange("(c p) n -> p c n", p=128)
        for ts in range(Tn // TS):
            for i in range(NKG):
                P.dma("sp", inb[i].apv.rearrange("p (c t) -> p c t", c=kpg),
                      srcv[:, i * kpg:(i + 1) * kpg, ts * TS:(ts + 1) * TS], w=[inb[i]], sbuf=inb[i])
            for g0 in range(0, ncols, GW):
                wb = wrot.get()
                w3 = wb.apv.rearrange("p (c n) -> p c n", c=KC)
                P.dma("pool", w3, Wv[:, :, c0 + g0:c0 + g0 + GW], w=[wb], sbuf=wb)
                for st_ in range(TS // 128):
                    bk = next_bank()
                    for kc in range(KC):
                        ib = inb[kc // kpg]
                        kl = kc % kpg
                        P.op("pe", lambda e, o=bk.apv[:, 0:GW], lh=ib.apv[:, kl * TS + st_ * 128: kl * TS + st_ * 128 + 128],
                             rh=w3[:, kc, :], st=(kc == 0), sp=(kc == KC - 1):
                             e.matmul(o, lh, rh, start=st, stop=sp), r=[wb, ib], w=[bk])
                    ob = orot.get()
                    eng = "act" if st_ % 2 else "dve"
                    if eng == "act":
                        P.op("act", lambda e, o=ob.apv, i=bk.apv[:, 0:GW]: e.activation(out=o, in_=i, func=AF.Copy), r=[bk], w=[ob])
                    else:
                        P.op("dve", lambda e, o=ob.apv, i=bk.apv[:, 0:GW]: e.tensor_copy(out=o, in_=i), r=[bk], w=[ob])
                    tok0 = ts * TS + st_ * 128
                    P.dma("sp", dst[tok0:tok0 + 128, g0:g0 + GW], ob.apv, r=[ob], sbuf=ob)
        mem.pop()

    def epi_store(route, tts):
        def setup():
            return dict(f=Rot("ep_f", 4, tts, F32), b=Rot("ep_b", 4, tts, BF16), k=[0])

        def epi(col0, t0, bk, ctx, pre=None):
            dst, kind = route(col0)
            ctx["k"][0] += 1
            ps = bk.apv[:, 0:tts]
            if kind == "f32":
                ob = ctx["f"].get()
                if ctx["k"][0] % 2:
                    P.op("act", lambda e, o=ob.apv, i=ps: e.activation(out=o, in_=i, func=AF.Copy), r=[bk], w=[ob])
                else:
                    P.op("dve", lambda e, o=ob.apv, i=ps: e.tensor_copy(out=o, in_=i), r=[bk], w=[ob])
            elif kind == "bf16":
                ob = ctx["b"].get()
                if ctx["k"][0] % 2:
                    P.op("act", lambda e, o=ob.apv, i=ps: e.activation(out=o, in_=i, func=AF.Copy), r=[bk], w=[ob])
                else:
                    P.op("dve", lambda e, o=ob.apv, i=ps: e.tensor_copy(out=o, in_=i), r=[bk], w=[ob])
            elif kind == "gelu":
                ob = ctx["f"].get()
                P.op("act", lambda e, o=ob.apv, i=ps: e.activation(out=o, in_=i, func=AF.Gelu), r=[bk], w=[ob])
            elif kind == "sigmoid":
                ob = ctx["f"].get()
                P.op("act", lambda e, o=ob.apv, i=ps: e.activation(out=o, in_=i, func=AF.Sigmoid), r=[bk], w=[ob])
            elif kind == "relu2":
                tb = ctx["f"].get()
                ob = ctx["b"].get()
                P.op("act", lambda e, o=tb.apv, i=ps: e.activation(out=o, in_=i, func=AF.Relu), r=[bk], w=[tb])
                P.op("dve", lambda e, o=ob.apv, i=tb.apv: e.tensor_tensor(out=o, in0=i, in1=i, op=ALU.mult), r=[tb], w=[ob])
            P.dma("sp", dst[:, t0:t0 + tts], ob.apv, r=[ob], sbuf=ob)
        return epi, setup

    def epi_gate(gate_src, dst, tts, addsrc=None, out_dt=F32):
        def setup():
            return dict(g=Rot("eg_g", 8, tts, F32), a=Rot("eg_a", 8, tts, F32) if addsrc is not None else None,
                        o=Rot("eg_o", 3, tts, out_dt))

        def pre(col0, t0, ctx):
            gb = ctx["g"].get()
            P.dma("sp", gb.apv, gate_src[col0:col0 + 128, t0:t0 + tts], w=[gb], sbuf=gb)
            ab = None
            if addsrc is not None:
                ab = ctx["a"].get()
                P.dma("sp", ab.apv, addsrc[col0:col0 + 128, t0:t0 + tts], w=[ab], sbuf=ab)
            return gb, ab

        def epi(col0, t0, bk, ctx, pre_):
            gb, ab = pre_
            ps = bk.apv[:, 0:tts]
            ob = ctx["o"].get()
            if addsrc is None:
                P.op("dve", lambda e, o=ob.apv, i=ps, g=gb.apv: e.tensor_tensor(out=o, in0=i, in1=g, op=ALU.mult), r=[bk, gb], w=[ob])
            else:
                P.op("dve", lambda e, o=gb.apv, i=ps, g=gb.apv: e.tensor_tensor(out=o, in0=i, in1=g, op=ALU.mult), r=[bk, gb], w=[gb])
                P.op("dve", lambda e, o=ob.apv, i=gb.apv, a_=ab.apv: e.tensor_tensor(out=o, in0=i, in1=a_, op=ALU.add), r=[gb, ab], w=[ob])
            P.dma("sp", dst[col0:col0 + 128, t0:t0 + tts], ob.apv, r=[ob], sbuf=ob)
        return epi, setup, pre

    def epi_resid(xold, xnew, tts):
        def setup():
            return dict(x=Rot("er_x", 8, tts, F32))

        def pre(col0, t0, ctx):
            xb = ctx["x"].get()
            P.dma("sp", xb.apv, xold[col0:col0 + 128, t0:t0 + tts], w=[xb], sbuf=xb)
            return xb

        def epi(col0, t0, bk, ctx, xb):
            P.op("dve", lambda e, o=xb.apv, i=bk.apv[:, 0:tts]: e.tensor_tensor(out=o, in0=i, in1=o, op=ALU.add), r=[bk, xb], w=[xb])
            P.dma("sp", xnew[col0:col0 + 128, t0:t0 + tts], xb.apv, r=[xb], sbuf=xb)
        return epi, setup, pre

    def qk_post(l, nq_tiles):
        mem.push()
        NB = 4
        z_rot = Rot("qk_z", 2 * NB, TT, F32)
        sq_rot = Rot("qk_sq", 2 * NB, TT, BF16, dma=False)
        zg_rot = Rot("qk_zg", 2 * NB, TT, F32, dma=False)
        zb_rot = Rot("qk_zb", 2 * NB, TT, BF16, dma=False)
        hr_rot = Rot("qk_hr", 2 * NB, TT, F32, dma=False)
        t1_rot = Rot("qk_t1", 2 * NB, TT, F32, dma=False)
        t2_rot = Rot("qk_t2", 2 * NB, TT, F32, dma=False)
        o_rot = Rot("qk_o", 2 * NB, TT, BF16)
        c_rot = Rot("qk_c", 2, TT, F32)
        s_rot = Rot("qk_s", 2, TT, F32)
        items_q = [(qz, qn, h, ("q_norm_g", l)) for h in range(NQH)]
        items_k = [(kz, kn, h, ("k_norm_g", l)) for h in range(NKV)]
        for tt in range(NTT):
            items = (items_q if tt < nq_tiles else []) + items_k
            t0 = tt * TT
            cb = c_rot.get()
            sb = s_rot.get()
            P.dma("sp", cb.apv, ropeC_in[:, t0:t0 + TT], w=[cb], sbuf=cb)
            P.dma("sp", sb.apv, ropeS_in[:, t0:t0 + TT], w=[sb], sbuf=sb)
            blist = [items[b0:b0 + NB] for b0 in range(0, len(items), NB)]
            zmap = {}

            def load_z(bi_):
                if bi_ < len(blist):
                    zz = [z_rot.get() for _ in range(len(blist[bi_]))]
                    for i_, (srcz_, dstn_, h_, gk_) in enumerate(blist[bi_]):
                        P.dma("sp", zz[i_].apv, srcz_[h_ * 128:(h_ + 1) * 128, t0:t0 + TT], w=[zz[i_]], sbuf=zz[i_])
                    zmap[bi_] = zz
            load_z(0)
            for bi_, batch in enumerate(blist):
                n = len(batch)
                load_z(bi_ + 1)
                zs = zmap.pop(bi_)
                sqs = [sq_rot.get() for _ in range(n)]
                for i in range(n):
                    P.op("act", lambda e, o=sqs[i].apv, i_=zs[i].apv: e.activation(out=o, in_=i_, func=AF.Square), r=[zs[i]], w=[sqs[i]])
                for i in range(n):
                    P.op("pe", lambda e, o=banks[i].apv[:, 0:TT], rh=sqs[i].apv: e.matmul(o, ones.apv, rh, start=True, stop=True), r=[sqs[i], ones], w=[banks[i]])
                zbs = [zb_rot.get() for _ in range(n)]
                for i, (srcz, dstn, h, gkey) in enumerate(batch):
                    P.op("act", lambda e, o=zbs[i].apv, i_=zs[i].apv, g=vcol(gkey): e.activation(out=o, in_=i_, func=AF.Identity, scale=g), r=[zs[i], vecs], w=[zbs[i]])
                for i in range(n):
                    P.op("pe", lambda e, o=banks[4 + i].apv[:, 0:TT], rh=zbs[i].apv: e.matmul(o, perm.apv, rh, start=True, stop=True), r=[zbs[i], perm], w=[banks[4 + i]])
                zgs = [zg_rot.get() for _ in range(n)]
                for i, (srcz, dstn, h, gkey) in enumerate(batch):
                    P.op("act", lambda e, o=zgs[i].apv, i_=zs[i].apv, g=vcol(gkey): e.activation(out=o, in_=i_, func=AF.Identity, scale=g), r=[zs[i], vecs], w=[zgs[i]])
                hrs = [hr_rot.get() for _ in range(n)]
                for i in range(n):
                    P.op("act", lambda e, o=hrs[i].apv, i_=banks[i].apv[:, 0:TT]: e.activation(out=o, in_=i_, func=AF.Ln, scale=1.0 / 128, bias=EPS_T.apv[:, 0:1]), r=[banks[i], EPS_T], w=[hrs[i]])
                for i in range(n):
                    P.op("act", lambda e, o=hrs[i].apv: e.activation(out=o, in_=o, func=AF.Exp, scale=-0.5), r=[hrs[i]], w=[hrs[i]])
                t1s = [t1_rot.get() for _ in range(n)]
                t2s = [t2_rot.get() for _ in range(n)]
                for i in range(n):
                    P.op("pool", lambda e, o=t1s[i].apv, i_=zgs[i].apv, c=cb.apv: e.tensor_tensor(out=o, in0=i_, in1=c, op=ALU.mult), r=[zgs[i], cb], w=[t1s[i]])
                for i in range(n):
                    P.op("dve", lambda e, o=t2s[i].apv, i_=banks[4 + i].apv[:, 0:TT], s_=sb.apv: e.tensor_tensor(out=o, in0=i_, in1=s_, op=ALU.mult), r=[banks[4 + i], sb], w=[t2s[i]])
                for i in range(n):
                    P.op("dve", lambda e, o=t2s[i].apv, a_=t1s[i].apv, b_=t2s[i].apv: e.tensor_tensor(out=o, in0=a_, in1=b_, op=ALU.add), r=[t1s[i], t2s[i]], w=[t2s[i]])
                for i, (srcz, dstn, h, gkey) in enumerate(batch):
                    ob = o_rot.get()
                    P.op("dve", lambda e, o=ob.apv, a_=t2s[i].apv, b_=hrs[i].apv: e.tensor_tensor(out=o, in0=a_, in1=b_, op=ALU.mult), r=[t2s[i], hrs[i]], w=[ob])
                    P.dma("sp", dstn[h * 128:(h + 1) * 128, t0:t0 + TT], ob.apv, r=[ob], sbuf=ob)
        mem.pop()

    def attention(n_qtiles):
        mem.push()
        SC = S // 128
        assert SC % 2 == 0
        NP = SC // 2
        scale = float(HD) ** -0.5
        kT = [mem.alloc("at_k%d" % h, S, BF16, dma=True) for h in range(NKV)]
        for h in range(NKV):
            P.dma("sp", kT[h].apv, kn[h * 128:(h + 1) * 128, :], w=[kT[h]], sbuf=kT[h])
        NVG = 4 if SC >= 4 else 1
        spg = SC // NVG
        vb = [mem.alloc("at_v%d" % i, spg * KVW, BF16, dma=True) for i in range(NVG)]
        vv = vtok.rearrange("(c p) n -> p c n", p=128)
        for i in range(NVG):
            P.dma("sp", vb[i].apv.rearrange("p (c n) -> p c n", c=spg), vv[:, i * spg:(i + 1) * spg, :], w=[vb[i]], sbuf=vb[i])
        q_rot = Rot("at_q", 2, T, BF16)
        p_rot = Rot("at_p", 3, 2 * TT, BF16, dma=False)
        rd_rot = Rot("at_rd", 2, TT, F32, dma=False)
        o_rot = Rot("at_o", 2, TT, BF16)
        bro = BankRot([0, 1])
        brd = BankRot([2, 3])
        spair = [pairs[2], pairs[3]]
        spi = [0]
        qbs = {}

        def load_q(hq_):
            if hq_ < NQH:
                qb_ = q_rot.get()
                P.dma("sp", qb_.apv[:, 0:n_qtiles * TT], qn[hq_ * 128:(hq_ + 1) * 128, 0:n_qtiles * TT], w=[qb_], sbuf=qb_)
                qbs[hq_] = qb_
        load_q(0)
        for hq in range(NQH):
            hk = hq // GROUP
            load_q(hq + 1)
            qb = qbs.pop(hq)
            for qt in range(n_qtiles):
                qs = qb.apv[:, qt * TT:(qt + 1) * TT]
                bo = bro.get()
                bd = brd.get()
                sp_ = [None] * NP

                def score(jp):
                    pr = spair[spi[0] % 2]
                    spi[0] += 1
                    sp_[jp] = pr
                    for hh in range(2):
                        j = 2 * jp + hh
                        P.op("pe", lambda e, o=pr.apv[:, hh * 512: hh * 512 + TT], lh=kT[hk].apv[:, j * 128:(j + 1) * 128], rh=qs:
                             e.matmul(o, lh, rh, start=True, stop=True), r=[kT[hk], qb], w=[pr])
                score(0)
                for jp in range(NP):
                    if jp + 1 < NP:
                        score(jp + 1)
                    pb = p_rot.get()
                    pr = sp_[jp]
                    if TT == 512:
                        P.op("act", lambda e, o=pb.apv, i=pr.apv: e.activation(out=o, in_=i, func=AF.Exp, scale=scale), r=[pr], w=[pb])
                    else:
                        for hh in range(2):
                            P.op("act", lambda e, o=pb.apv[:, hh * TT:(hh + 1) * TT], i=pr.apv[:, hh * 512: hh * 512 + TT]:
                                 e.activation(out=o, in_=i, func=AF.Exp, scale=scale), r=[pr], w=[pb])
                    for hh in range(2):
                        j = 2 * jp + hh
                        vbuf = vb[j // spg]
                        vl = (j % spg) * KVW + hk * 128
                        ph = pb.apv[:, hh * TT:(hh + 1) * TT]
                        P.op("pe", lambda e, o=bo.apv[:, 0:TT], lh=vbuf.apv[:, vl:vl + 128], rh=ph, st=(j == 0), sp=(j == SC - 1):
                             e.matmul(o, lh, rh, start=st, stop=sp), r=[vbuf, pb], w=[bo])
                        P.op("pe", lambda e, o=bd.apv[:, 0:TT], rh=ph, st=(j == 0), sp=(j == SC - 1):
                             e.matmul(o, ones.apv, rh, start=st, stop=sp), r=[ones, pb], w=[bd])
                rd = rd_rot.get()
                P.op("dve", lambda e, o=rd.apv, i=bd.apv[:, 0:TT]: e.reciprocal(out=o, in_=i), r=[bd], w=[rd])
                ob = o_rot.get()
                P.op("dve", lambda e, o=ob.apv, i=bo.apv[:, 0:TT], r_=rd.apv: e.tensor_tensor(out=o, in0=i, in1=r_, op=ALU.mult), r=[bo, rd], w=[ob])
                P.dma("sp", oattn[hq * 128:(hq + 1) * 128, qt * TT:(qt + 1) * TT], ob.apv, r=[ob], sbuf=ob)
        mem.pop()

    def lru(l, out_T):
        mem.push()
        GB = min(4, NTT)
        GW_ = GB * TT
        upA = mem.alloc("lr_uA", H + 3, F32, dma=True)
        upB = mem.alloc("lr_uB", H + 3, F32, dma=True)
        ufs = [mem.alloc("lr_uf%d" % i, T, F32) for i in range(2)]
        ubs = [mem.alloc("lr_ub%d" % i, T, BF16, dma=True) for i in range(2)]
        ufh = [[mem.view("lr_uf%d%d" % (i, h), ufs[i], h * H, (h + 1) * H) for h in range(2)] for i in range(2)]
        ubh = [[mem.view("lr_ub%d%d" % (i, h), ubs[i], h * H, (h + 1) * H) for h in range(2)] for i in range(2)]
        afull = mem.alloc("lr_a", T, F32)
        bfull = mem.alloc("lr_b", T, F32)
        hf = mem.alloc("lr_hf", T, F32)
        hb = mem.alloc("lr_hb", T, F32)
        gyb = mem.alloc("lr_gy", out_T, F32, dma=True)
        ini = mem.alloc("lr_ini", 8, F32)
        nbat = NTT // GB
        a_t = [mem.view("lr_a%d" % i, afull, i * GW_, (i + 1) * GW_) for i in range(nbat)]
        b_t = [mem.view("lr_b%d" % i, bfull, i * GW_, (i + 1) * GW_) for i in range(nbat)]
        wg = [[mem.alloc("lr_w%d%d" % (d, g), 128, BF16, dma="sw") for g in range(2)] for d in range(2)]
        r_rot = Rot("lr_r", 2, GW_, F32, dma=False)
        i_rot = Rot("lr_i", 1, GW_, F32, dma=False)
        e_rot = Rot("lr_e", 1, GW_, F32, dma=False)
        npair = max(1, GB // 2)

        def tsmul(o, i, f, rbufs, wbufs):
            P.op("dve", lambda e, o=o, i=i, f=f: e.tensor_scalar(out=o, in0=i, scalar1=f, scalar2=None, op0=ALU.mult), r=rbufs + [flags], w=wbufs)

        def scan(o, a_, b_, init, rbufs, wbufs, rev=False):
            if rev:
                o, a_, b_ = rev_ap(o), rev_ap(a_), rev_ap(b_)
            P.op("dve", lambda e, o=o, a_=a_, b_=b_, init=init: e.tensor_tensor_scan(out=o, data0=a_, data1=b_, initial=init, op0=ALU.mult, op1=ALU.add),
                 r=rbufs, w=wbufs)

        def load_u(c):
            P.dma("sp", upA.apv[:, 2:H + 2], ud[c * 128:(c + 1) * 128, 0:H], w=[upA], sbuf=upA)
            P.dma("sp", upB.apv[:, 2:H + 2], ud[c * 128:(c + 1) * 128, H:T], w=[upB], sbuf=upB)

        def conv_act1(c, s_):
            tsmul(upA.apv[:, 0:2], upB.apv[:, H:H + 2], fB, [upB], [upA])
            tsmul(upA.apv[:, H + 2:H + 3], upB.apv[:, 2:3], fA, [upB], [upA])
            tsmul(upB.apv[:, 0:2], upA.apv[:, H:H + 2], fA, [upA], [upB])
            tsmul(upB.apv[:, H + 2:H + 3], upA.apv[:, 2:3], fB, [upA], [upB])
            for h_, up_ in enumerate((upA, upB)):
                P.op("act", lambda e, o=ufh[s_][h_].apv, i=up_.apv[:, 0:H], w0=vcol(("conv_w", l, 0), c), b=vcol(("conv_b", l), c):
                     e.activation(out=o, in_=i, func=AF.Identity, scale=w0, bias=b), r=[up_, vecs], w=[ufh[s_][h_]])

        def conv_dve(c, s_):
            for h_, up_ in enumerate((upA, upB)):
                for k in range(1, 4):
                    P.op("dve", lambda e, o=ufh[s_][h_].apv, i=up_.apv[:, k:k + H], wk=vcol(("conv_w", l, k), c):
                         e.scalar_tensor_tensor(out=o, in0=i, scalar=wk, in1=o, op0=ALU.mult, op1=ALU.add), r=[up_, ufh[s_][h_], vecs], w=[ufh[s_][h_]])

        def conv_act2(c, s_):
            for h_ in range(2):
                P.op("act", lambda e, o=ubh[s_][h_].apv, i=ufh[s_][h_].apv: e.activation(out=o, in_=i, func=AF.Copy), r=[ufh[s_][h_]], w=[ubh[s_][h_]])

        def gates(c, d, s_):
            k = clcol(l, d, c)
            uf_, ub_ = ufs[s_], ubs[s_]
            for bi in range(nbat):
                c0 = bi * GW_
                for ti in range(GB):
                    pr_r = pairs[ti // 2]
                    pr_i = pairs[2 + ti // 2]
                    hh = ti % 2
                    rh = ub_.apv[:, c0 + ti * TT: c0 + (ti + 1) * TT]
                    P.op("pe", lambda e, o=pr_r.apv[:, hh * 512: hh * 512 + TT], lh=wg[d][0].apv, rh=rh: e.matmul(o, lh, rh, start=True, stop=True), r=[wg[d][0]] + ubh[s_], w=[pr_r])
                    P.op("pe", lambda e, o=pr_i.apv[:, hh * 512: hh * 512 + TT], lh=wg[d][1].apv, rh=rh: e.matmul(o, lh, rh, start=True, stop=True), r=[wg[d][1]] + ubh[s_], w=[pr_i])
                rb = r_rot.get()
                ib_ = i_rot.get()
                eb = e_rot.get()
                for (dst_, pbase, bkey) in ((rb, 0, ("lru_b_r", l, d)), (ib_, 2, ("lru_b_i", l, d))):
                    if TT == 512 and GB >= 2:
                        for pi in range(npair):
                            P.op("act", lambda e, o=dst_.apv[:, pi * 1024:(pi + 1) * 1024], i=pairs[pbase + pi].apv, b=vcol(bkey, c):
                                 e.activation(out=o, in_=i, func=AF.Sigmoid, bias=b), r=[pairs[pbase + pi], vecs], w=[dst_])
                    else:
                        for ti in range(GB):
                            P.op("act", lambda e, o=dst_.apv[:, ti * TT:(ti + 1) * TT], i=pairs[pbase + ti // 2].apv[:, (ti % 2) * 512:(ti % 2) * 512 + TT], b=vcol(bkey, c):
                                 e.activation(out=o, in_=i, func=AF.Sigmoid, bias=b), r=[pairs[pbase + ti // 2], vecs], w=[dst_])
                P.op("act", lambda e, o=a_t[bi].apv, i=rb.apv, sc_=cl.apv[:, k:k + 1]: e.activation(out=o, in_=i, func=AF.Exp, scale=sc_), r=[rb, cl], w=[a_t[bi]])
                P.op("act", lambda e, o=eb.apv, i=rb.apv, sc_=cl2.apv[:, k:k + 1]: e.activation(out=o, in_=i, func=AF.Exp, scale=sc_), r=[rb, cl2], w=[eb])
                P.op("act", lambda e, o=eb.apv: e.activation(out=o, in_=o, func=AF.Sqrt, scale=-1.0, bias=ONE_T.apv[:, 0:1]), r=[eb, ONE_T], w=[eb])
                P.op("dve", lambda e, o=ib_.apv, u_=uf_.apv[:, c0:c0 + GW_]: e.tensor_tensor(out=o, in0=o, in1=u_, op=ALU.mult), r=[ib_] + ufh[s_], w=[ib_])
                P.op("dve", lambda e, o=b_t[bi].apv, i=ib_.apv, s2=eb.apv: e.tensor_tensor(out=o, in0=i, in1=s2, op=ALU.mult), r=[ib_, eb], w=[b_t[bi]])

        def scans(d):
            ab_ = a_t + b_t
            aA, aB = afull.apv[:, 0:H], afull.apv[:, H:T]
            bA, bB = bfull.apv[:, 0:H], bfull.apv[:, H:T]
            if d == 0:
                scan(hf.apv[:, H:T], aB, bB, 0.0, ab_, [hf])
                tsmul(ini.apv[:, 0:1], hf.apv[:, T - 1:T], fB, [hf], [ini])
                scan(hf.apv[:, 0:H], aA, bA, ini.apv[:, 0:1], ab_ + [ini], [hf])
                if out_T > H:
                    tsmul(ini.apv[:, 1:2], hf.apv[:, H - 1:H], fA, [hf], [ini])
                    scan(hf.apv[:, H:T], aB, bB, ini.apv[:, 1:2], ab_ + [ini], [hf])
            else:
                scan(hb.apv[:, 0:H], aA, bA, 0.0, ab_, [hb], rev=True)
                tsmul(ini.apv[:, 2:3], hb.apv[:, 0:1], fB, [hb], [ini])
                scan(hb.apv[:, H:T], aB, bB, ini.apv[:, 2:3], ab_ + [ini], [hb], rev=True)
                tsmul(ini.apv[:, 3:4], hb.apv[:, H:H + 1], fA, [hb], [ini])
                scan(hb.apv[:, 0:H], aA, bA, ini.apv[:, 3:4], ab_ + [ini], [hb], rev=True)

        load_u(0)
        conv_act1(0, 0)
        conv_dve(0, 0)
        conv_act2(0, 0)
        for c in range(DC):
            s_ = c % 2
            n_ = 1 - s_
            more = c + 1 < DC
            P.dma("sp", gyb.apv, gy[c * 128:(c + 1) * 128, 0:out_T], w=[gyb], sbuf=gyb)
            for d in range(2):
                P.dma("pool", wg[d][0].apv, lru_w_r[l, d, c], w=[wg[d][0]], sbuf=wg[d][0])
                P.dma("pool", wg[d][1].apv, lru_w_i[l, d, c], w=[wg[d][1]], sbuf=wg[d][1])
            if more:
                load_u(c + 1)
            gates(c, 0, s_)
            if more:
                conv_act1(c + 1, n_)
            scans(0)
            if more:
                conv_dve(c + 1, n_)
            gates(c, 1, s_)
            if more:
                conv_act2(c + 1, n_)
            scans(1)
            P.op("dve", lambda e, o=hf.apv[:, 0:out_T], b=hb.apv[:, 0:out_T]: e.tensor_tensor(out=o, in0=o, in1=b, op=ALU.add), r=[hf, hb], w=[hf])
            ob = ubs[s_]
            P.op("dve", lambda e, o=ob.apv[:, 0:out_T], a=hf.apv[:, 0:out_T], g=gyb.apv: e.tensor_tensor(out=o, in0=a, in1=g, op=ALU.mult), r=[hf, gyb] + ubh[s_], w=[ob] + ubh[s_])
            P.dma("sp", ornn[c * 128:(c + 1) * 128, 0:out_T], ob.apv[:, 0:out_T], r=[ob] + ubh[s_], sbuf=ob)
        mem.pop()

    def cross_attention(n_qtiles):
        mem.push()
        MC = M // 128
        scale = float(XHD) ** -0.5
        kb = mem.alloc("ca_k", DC * M, BF16, dma=True)
        P.dma("sp", kb.apv.rearrange("p (c m) -> p c m", c=DC), xk.rearrange("(c p) m -> p c m", p=128), w=[kb], sbuf=kb)
        vb_ = mem.alloc("ca_v", MC * D, BF16, dma=True)
        P.dma("sp", vb_.apv.rearrange("p (c n) -> p c n", c=MC), xvtok.rearrange("(c p) n -> p c n", p=128), w=[vb_], sbuf=vb_)
        q_rot = Rot("ca_q", 2, DC * TT, BF16)
        p_rot = Rot("ca_p", 2 * MC, TT, BF16, dma=False)
        rd_rot = Rot("ca_rd", 2, TT, F32, dma=False)
        o_rot = Rot("ca_o", 3, TT, BF16)
        xqv = xq.rearrange("(c p) t -> p c t", p=128)
        brs = BankRot([0, 1, 2, 3])
        brd = BankRot([4, 5])
        bro = BankRot([6, 7])
        for qt in range(n_qtiles):
            t0 = qt * TT
            qb = q_rot.get()
            P.dma("sp", qb.apv.rearrange("p (c t) -> p c t", c=DC), xqv[:, :, t0:t0 + TT], w=[qb], sbuf=qb)
            for h in range(NXH):
                pbs = []
                for mc in range(MC):
                    bs = brs.get()
                    for dc in range(XDC):
                        ch = h * XDC + dc
                        P.op("pe", lambda e, o=bs.apv[:, 0:TT], lh=kb.apv[:, ch * M + mc * 128: ch * M + mc * 128 + 128],
                             rh=qb.apv[:, ch * TT:(ch + 1) * TT], st=(dc == 0), sp=(dc == XDC - 1):
                             e.matmul(o, lh, rh, start=st, stop=sp), r=[kb, qb], w=[bs])
                    pb = p_rot.get()
                    P.op("act", lambda e, o=pb.apv, i=bs.apv[:, 0:TT]: e.activation(out=o, in_=i, func=AF.Exp, scale=scale), r=[bs], w=[pb])
                    pbs.append(pb)
                bd = brd.get()
                for mc in range(MC):
                    P.op("pe", lambda e, o=bd.apv[:, 0:TT], rh=pbs[mc].apv, st=(mc == 0), sp=(mc == MC - 1):
                         e.matmul(o, ones.apv, rh, start=st, stop=sp), r=[ones, pbs[mc]], w=[bd])
                rd = rd_rot.get()
                P.op("dve", lambda e, o=rd.apv, i=bd.apv[:, 0:TT]: e.reciprocal(out=o, in_=i), r=[bd], w=[rd])
                for dc in range(XDC):
                    ch = h * XDC + dc
                    bo = bro.get()
                    for mc in range(MC):
                        P.op("pe", lambda e, o=bo.apv[:, 0:TT], lh=vb_.apv[:, mc * D + ch * 128: mc * D + ch * 128 + 128], rh=pbs[mc].apv,
                             st=(mc == 0), sp=(mc == MC - 1): e.matmul(o, lh, rh, start=st, stop=sp), r=[vb_, pbs[mc]], w=[bo])
                    ob = o_rot.get()
                    P.op("dve", lambda e, o=ob.apv, i=bo.apv[:, 0:TT], r_=rd.apv: e.tensor_tensor(out=o, in0=i, in1=r_, op=ALU.mult), r=[bo, rd], w=[ob])
                    P.dma("sp", xo[ch * 128:(ch + 1) * 128, t0:t0 + TT], ob.apv, r=[ob], sbuf=ob)
        mem.pop()

    EPS_T = mem.alloc("eps_t", 8, F32)
    ONE_T = mem.alloc("one_t", 8, F32)
    P.op("dve", lambda e: e.memset(EPS_T.apv, EPS), w=[EPS_T])
    P.op("dve", lambda e: e.memset(ONE_T.apv, 1.0), w=[ONE_T])
    P.barrier()

    xcur = xT_in
    xi = 0
    for l in range(L):
        Tq = TQL if l == L - 1 else T
        NQT = Tq // TT
        prep_norm(xcur, ("mix_norm_g", l), xn, T, PT)

        def route_in(col0):
            if col0 < o_k:
                return qz[col0 - o_q: col0 - o_q + 128], "f32"
            if col0 < o_v:
                return kz[col0 - o_k: col0 - o_k + 128], "f32"
            if col0 < o_y:
                return ud[col0 - o_u: col0 - o_u + 128], "f32"
            if col0 < o_ga:
                return gy[col0 - o_y: col0 - o_y + 128], "gelu"
            if col0 < o_gr:
                return sga[col0 - o_ga: col0 - o_ga + 128], "sigmoid"
            return sgr[col0 - o_gr: col0 - o_gr + 128], "sigmoid"
        ep, su = epi_store(route_in, TT)
        if Tq == T:
            linear(xn, D, w_in[l], [(0, o_v), (o_u, NIN)], T, TT, ep, su)
        else:
            linear(xn, D, w_in[l], [(o_k, o_v), (o_u, o_y)], T, TT, ep, su)
            linear(xn, D, w_in[l], [(0, o_k), (o_y, NIN)], Tq, TT, ep, su)
        linear_tm(xn, D, w_in[l], o_v, KVW, T, vtok)
        qk_post(l, NQT)
        attention(NQT)
        lru(l, Tq)
        ep, su, pr = epi_gate(sga, tmpA, TT)
        linear(oattn, AW, w_ab[l], [(0, D)], Tq, TT, ep, su, epi_pre=pr)
        ep, su, pr = epi_gate(sgr, merged, TT, addsrc=tmpA, out_dt=BF16)
        linear(ornn, D, w_rb[l], [(0, D)], Tq, TT, ep, su, epi_pre=pr)
        xnext = xs[xi % 2]; xi += 1
        ep, su, pr = epi_resid(xcur, xnext, TT)
        linear(merged, D, w_mo[l], [(0, D)], Tq, TT, ep, su, epi_pre=pr)
        xcur = xnext
        prep_norm(xcur, ("cross_norm_g", l), xn, Tq, PT)
        ep, su = epi_store(lambda c0: (xq[c0:c0 + 128], "bf16"), TT)
        linear(xn, D, w_xq[l], [(0, D)], Tq, TT, ep, su)
        prep_norm(memT_in, ("mem_norm_g", l), mn, M, min(MT, PT))
        ep, su = epi_store(lambda c0: (xk[c0:c0 + 128], "bf16"), MT)
        linear(mn, D, w_xkv[l], [(0, D)], M, MT, ep, su)
        linear_tm(mn, D, w_xkv[l], D, D, M, xvtok)
        cross_attention(NQT)
        xnext = xs[xi % 2]; xi += 1
        ep, su, pr = epi_resid(xcur, xnext, TT)
        linear(xo, D, w_xo[l], [(0, D)], Tq, TT, ep, su, epi_pre=pr)
        xcur = xnext
        prep_norm(xcur, ("mlp_norm_g", l), xn, Tq, PT)
        ep, su = epi_store(lambda c0: (hid[c0:c0 + 128], "relu2"), TT)
        linear(xn, D, w_up[l], [(0, DFF)], Tq, TT, ep, su)
        xnext = xs[xi % 2]; xi += 1
        ep, su, pr = epi_resid(xcur, xnext, TT)
        linear(hid, DFF, None, [(0, D)], Tq, TT, ep, su, Wtiled=w_down_t[l], epi_pre=pr)
        xcur = xnext
    prep_norm(xcur, ("final_norm_g",), outT, TQL, PT, final=True)
    P.barrier()

    P.finalize()
    with nc.Block() as block:
        @block.sync
        def _(e):
            P.emit("sp", e)

        @block.tensor
        def _(e):
            P.emit("pe", e)

        @block.scalar
        def _(e):
            P.emit("act", e)

        @block.vector
        def _(e):
            P.emit("dve", e)

        @block.gpsimd
        def _(e):
            P.emit("pool", e)
    stack.close()
    return nc, P


def rope_tables(cfg):
    S, GW = cfg["S"], cfg["GRID_W"]
    rows_n = S // GW
    row = np.repeat(np.arange(rows_n, dtype=np.float32), GW)
    col = np.tile(np.arange(GW, dtype=np.float32), rows_n)
    n_freq = 32
    inv = (np.float32(ROPE_THETA) ** (-np.arange(n_freq, dtype=np.float32) / np.float32(n_freq))).astype(np.float32)
    ang_r = (row[:, None] * inv[None, :]).astype(np.float32)
    ang_c = (col[:, None] * inv[None, :]).astype(np.float32)
    cr, sr, cc, sc = np.cos(ang_r), np.sin(ang_r), np.cos(ang_c), np.sin(ang_c)
    C = np.concatenate([cr, cr, cc, cc], axis=1).T
    Sg = np.concatenate([-sr, sr, -sc, sc], axis=1).T
    return np.ascontiguousarray(C, dtype=np.float32), np.ascontiguousarray(Sg, dtype=np.float32)


def perm_matrix():
    Pm = np.zeros((128, 128), np.float32)
    for blk in range(2):
        b = blk * 64
        for i in range(32):
            Pm[b + i, b + 32 + i] = 1.0
            Pm[b + 32 + i, b + i] = 1.0
    return Pm


def pack_vecs(cfg, inp):
    cols, NV = vec_layout(cfg)
    L, D = cfg["L"], cfg["D"]
    DC = D // 128
    V = np.zeros((128, NV), np.float32)

    def put(key, vec):
        v = np.asarray(vec, np.float32)
        k = v.size // 128
        V[:, cols[key]:cols[key] + k] = v.reshape(k, 128).T

    for l in range(L):
        for nm in ("mix_norm_g", "cross_norm_g", "mem_norm_g", "mlp_norm_g"):
            put((nm, l), inp[nm][l])
        put(("q_norm_g", l), inp["q_norm_g"][l])
        put(("k_norm_g", l), inp["k_norm_g"][l])
        for k in range(4):
            put(("conv_w", l, k), inp["conv_w"][l, k])
        put(("conv_b", l), inp["conv_b"][l])
        for d in range(2):
            put(("lru_b_r", l, d), inp["lru_b_r"][l, d])
            put(("lru_b_i", l, d), inp["lru_b_i"][l, d])
            put(("lru_lambda", l, d), inp["lru_lambda"][l, d])
    put(("final_norm_g",), inp["final_norm_g"])
    return V


def make_in_maps(cfg, inp, cores):
    C, Sg = rope_tables(cfg)
    Pm = perm_matrix()
    V = pack_vecs(cfg, inp)
    S = cfg["S"]
    Hh = S // 2
    shared = {"vecs": V, "perm": Pm}
    for k in ("w_in", "lru_w_r", "lru_w_i", "w_attn_branch", "w_rnn_branch", "w_mix_out", "w_xq", "w_xkv", "w_xo", "w_up"):
        shared[k] = np.ascontiguousarray(np.asarray(inp[k], np.float32))
    wd = np.asarray(inp["w_down"], np.float32)
    L_, DFF_, D_ = wd.shape
    shared["w_down_t"] = np.ascontiguousarray(
        wd.reshape(L_, DFF_ // 128, 128, D_ // 128, 128).transpose(0, 3, 2, 1, 4).reshape(L_, D_ // 128, 128, DFF_))
    maps = []
    for (b, hf) in cores:
        idx = np.concatenate([np.arange(hf * Hh, hf * Hh + Hh), np.arange((1 - hf) * Hh, (1 - hf) * Hh + Hh)])
        m = dict(shared)
        m["xT"] = np.ascontiguousarray(np.asarray(inp["x"][b], np.float32)[idx].T)
        m["memT"] = np.ascontiguousarray(np.asarray(inp["mem"][b], np.float32).T)
        m["ropeC"] = np.ascontiguousarray(C[:, idx])
        m["ropeS"] = np.ascontiguousarray(Sg[:, idx])
        fl = np.zeros((128, 2), np.float32)
        fl[:, 0] = 1.0 - hf
        fl[:, 1] = float(hf)
        m["flags"] = fl
        maps.append(m)
    return maps


_NC_CACHE = {}


def kernel(**inputs):
    cfg = FULL_CFG
    B = inputs["x"].shape[0]
    S = cfg["S"]
    if "nc" not in _NC_CACHE:
        _NC_CACHE["nc"] = build_program(cfg)[0]
    nc = _NC_CACHE["nc"]
    if cfg.get("SPLIT2"):
        cores = [(b, hf) for b in range(B) for hf in range(2)]
    else:
        cores = [(b, 0) for b in range(B)]
    in_maps = make_in_maps(cfg, inputs, cores)
    res = run_bass_kernel_spmd(nc, in_maps, core_ids=list(range(len(cores))))
    out = np.zeros((B, S, cfg["D"]), np.float32)
    Hh = S // 2
    for (b, hf), r in zip(cores, res.results):
        o = np.ascontiguousarray(r["outT"].T)
        if cfg.get("SPLIT2"):
            out[b, hf * Hh:(hf + 1) * Hh] = o
        else:
            out[b] = o
    return out
```

```python
import numpy as np
import concourse.bass as bass
import concourse.mybir as mybir
from concourse.bass_utils import run_bass_kernel_spmd

F32 = mybir.dt.float32
BF16 = mybir.dt.bfloat16
AF = mybir.ActivationFunctionType
ALU = mybir.AluOpType

EPS = 1e-6
LRU_C = 8.0
ROPE_THETA = 10000.0

FULL_CFG = dict(D=2048, S=4096, NQH=16, NKV=4, M=256, NXH=4, DFF=8192, L=2, GRID_W=64, SPLIT2=True)


class Buf:
    __slots__ = ("name", "writer", "readers", "sem", "apv")

    def __init__(self, name, apv=None, sem=None):
        self.name = name
        self.writer = None
        self.readers = []
        self.sem = sem
        self.apv = apv


class Op:
    __slots__ = ("eng", "fn", "deps", "signal", "ev", "is_dma", "idx")

    def __init__(self, eng, fn):
        self.eng = eng
        self.fn = fn
        self.deps = []
        self.signal = False
        self.ev = None
        self.is_dma = False


ENGS = ("sp", "pe", "act", "dve", "pool")


class Prog:
    def __init__(self, nc, n_dma_sems=48):
        self.nc = nc
        self.ops = {e: [] for e in ENGS}
        self.esem = {}
        self.stack = None
        self.dma_sems = []
        self.dma_cnt = {}
        self.free_dma_sems = []
        self.stage_dma_last = {}
        self.pending_bar = {e: None for e in ENGS}
        self.n_dma_sems = n_dma_sems
        self.stage_bufs = []
        self.nops = 0

    def setup_sems(self, stack):
        for e in ("pe", "act", "dve", "pool"):
            self.esem[e] = stack.enter_context(self.nc.semaphore("s_" + e))
        self.bar_sem = stack.enter_context(self.nc.semaphore("s_bar"))
        self.bar_cnt = 0
        for i in range(self.n_dma_sems):
            s = stack.enter_context(self.nc.semaphore("s_dma%d" % i))
            self.dma_sems.append(s)
            self.dma_cnt[id(s)] = 0
        n_sw = 12
        self.free_sw_sems = list(self.dma_sems[:n_sw])
        self.free_dma_sems = list(self.dma_sems[n_sw:])
        self.sw_ids = set(id(s) for s in self.free_sw_sems)

    def get_dma_sem(self, sw=False):
        return self.free_sw_sems.pop() if sw else self.free_dma_sems.pop()

    def put_dma_sem(self, s):
        (self.free_sw_sems if id(s) in self.sw_ids else self.free_dma_sems).append(s)

    def op(self, eng, fn, r=(), w=()):
        o = Op(eng, fn)
        deps = []
        for b in r:
            if b.writer is not None:
                deps.append(b.writer)
        for b in w:
            if b.writer is not None:
                deps.append(b.writer)
            deps.extend(b.readers)
        if self.pending_bar[eng] is not None:
            deps.append(self.pending_bar[eng])
            self.pending_bar[eng] = None
        seen = set()
        for d in deps:
            if d is o or id(d) in seen:
                continue
            seen.add(id(d))
            if d.eng == "pe" and eng == "pe" and not d.is_dma:
                continue
            d.signal = True
            o.deps.append(d)
        for b in w:
            b.writer = o
            b.readers = []
        for b in r:
            rl = b.readers
            if rl and (not rl[-1].is_dma) and rl[-1].eng == eng:
                rl[-1] = o
            else:
                rl.append(o)
        self.ops[eng].append(o)
        self.nops += 1
        return o

    def dma(self, eng, out_ap, in_ap, r=(), w=(), sbuf=None):
        assert sbuf is not None and sbuf.sem is not None, "dma needs an sbuf Buf with a semaphore"
        assert (id(sbuf.sem) in self.sw_ids) == (eng == "pool"), "semaphore pool / DMA queue mismatch"
        o = self.op(eng, lambda e: e.dma_start(out=out_ap, in_=in_ap), r=r, w=w)
        o.is_dma = True
        o.signal = True
        sid = id(sbuf.sem)
        self.dma_cnt[sid] += 16
        o.ev = (sbuf.sem, self.dma_cnt[sid])
        self.stage_dma_last[sid] = o
        return o

    def barrier(self):
        deps = list(self.stage_dma_last.values())
        for e in ("pe", "act", "dve", "pool"):
            for o_ in reversed(self.ops[e]):
                if not o_.is_dma:
                    deps.append(o_)
                    break
        self.bar_cnt += 1
        cnt = self.bar_cnt
        bs = self.bar_sem
        o = Op("sp", lambda e: e.sem_inc(bs, 1))
        for d in deps:
            d.signal = True
            o.deps.append(d)
        if self.pending_bar["sp"] is not None:
            self.pending_bar["sp"] = None
        o.ev = (bs, cnt)
        o.is_dma = True
        o.signal = False
        self.ops["sp"].append(o)
        for e in ("pe", "act", "dve", "pool"):
            self.pending_bar[e] = o
        self.stage_dma_last = {}
        return o

    def finalize(self):
        for e in ("pe", "act", "dve", "pool"):
            c = 0
            for o in self.ops[e]:
                if o.is_dma:
                    continue
                if o.signal:
                    c += 1
                    o.ev = (self.esem[e], c)

    def emit(self, eng, handle):
        waited = {}
        for o in self.ops[eng]:
            need = {}
            for d in o.deps:
                sem, cnt = d.ev
                k = id(sem)
                if waited.get(k, 0) >= cnt:
                    continue
                if k not in need or need[k][1] < cnt:
                    need[k] = (sem, cnt)
            for k, (sem, cnt) in need.items():
                handle.wait_ge(sem, cnt)
                waited[k] = cnt
            ins = o.fn(handle)
            if o.is_dma:
                if o.ev[0] is not self.bar_sem:
                    ins.then_inc(o.ev[0], 16)
            elif o.signal:
                ins.then_inc(o.ev[0], 1)


class Mem:
    def __init__(self, prog, big_ap, nwords):
        self.P = prog
        self.big = big_ap
        self.nwords = nwords
        self.top = 0
        self.marks = []
        self.stage_sems = []

    def push(self):
        self.marks.append((self.top, len(self.stage_sems)))

    def pop(self):
        self.P.barrier()
        top, ns = self.marks.pop()
        self.top = top
        while len(self.stage_sems) > ns:
            self.P.put_dma_sem(self.stage_sems.pop())

    def alloc(self, name, nelem, dt=F32, dma=False):
        nw = nelem if dt == F32 else (nelem + 1) // 2
        nw = (nw + 7) // 8 * 8
        off = self.top
        self.top += nw
        assert self.top <= self.nwords, "SBUF overflow at %s: %d > %d" % (name, self.top, self.nwords)
        ap = self.big[:, off:off + nw]
        if dt != F32:
            ap = ap.bitcast(dt)
        ap = ap[:, 0:nelem]
        sem = None
        if dma:
            sem = self.P.get_dma_sem(sw=(dma == "sw"))
            self.stage_sems.append(sem)
        return Buf(name, apv=ap, sem=sem)

    def view(self, name, parent, lo, hi, dma=False):
        sem = None
        if dma:
            sem = self.P.get_dma_sem()
            self.stage_sems.append(sem)
        return Buf(name, apv=parent.apv[:, lo:hi], sem=sem)


def rev_ap(ap2d):
    p, f = ap2d.ap[0], ap2d.ap[1]
    n = f[1]
    return bass.AP(ap2d.tensor, ap2d.offset + (n - 1) * f[0], [list(p), [-f[0], n]])


def vec_layout(cfg):
    D, L = cfg["D"], cfg["L"]
    DC = D // 128
    cols = {}
    n = 0

    def add(key, k):
        nonlocal n
        cols[key] = n
        n += k

    for l in range(L):
        for nm in ("mix_norm_g", "cross_norm_g", "mem_norm_g", "mlp_norm_g"):
            add((nm, l), DC)
        add(("q_norm_g", l), 1)
        add(("k_norm_g", l), 1)
        for k in range(4):
            add(("conv_w", l, k), DC)
        add(("conv_b", l), DC)
        for d in range(2):
            add(("lru_b_r", l, d), DC)
            add(("lru_b_i", l, d), DC)
            add(("lru_lambda", l, d), DC)
    add(("final_norm_g",), DC)
    return cols, n


def build_program(cfg):
    D, S, NQH, NKV, M, NXH, DFF, L = (cfg[k] for k in ("D", "S", "NQH", "NKV", "M", "NXH", "DFF", "L"))
    T = S
    H = T // 2
    SPLIT2 = bool(cfg.get("SPLIT2", False))
    TQL = H if SPLIT2 else T
    DC = D // 128
    HD = 128
    AW = NQH * HD
    KVW = NKV * HD
    GROUP = NQH // NKV
    assert AW == D
    XHD = D // NXH
    XDC = XHD // 128
    NIN = AW + 2 * KVW + 2 * D + 2 * D
    o_q, o_k, o_v, o_u, o_y, o_ga, o_gr = 0, AW, AW + KVW, AW + 2 * KVW, AW + 2 * KVW + D, AW + 2 * KVW + 2 * D, AW + 2 * KVW + 3 * D
    TT = min(512, T)
    NTT = T // TT
    MT = min(512, M)
    PT = min(512, T)
    vcols, NV = vec_layout(cfg)

    nc = bass.Bass("TRN2", target_bir_lowering=False)

    def din(name, shape, dt=F32):
        return nc.dram_tensor(name, list(shape), dt, kind="ExternalInput").ap()

    def dscr(name, shape, dt):
        return nc.dram_tensor(name, list(shape), dt, kind="Internal").ap()

    xT_in = din("xT", [D, T])
    memT_in = din("memT", [D, M])
    vecs_in = din("vecs", [128, NV])
    ropeC_in = din("ropeC", [128, S])
    ropeS_in = din("ropeS", [128, S])
    perm_in = din("perm", [128, 128])
    flags_in = din("flags", [128, 2])
    w_in = din("w_in", [L, D, NIN])
    lru_w_r = din("lru_w_r", [L, 2, DC, 128, 128])
    lru_w_i = din("lru_w_i", [L, 2, DC, 128, 128])
    w_ab = din("w_attn_branch", [L, AW, D])
    w_rb = din("w_rnn_branch", [L, D, D])
    w_mo = din("w_mix_out", [L, D, D])
    w_xq = din("w_xq", [L, D, D])
    w_xkv = din("w_xkv", [L, D, 2 * D])
    w_xo = din("w_xo", [L, D, D])
    w_up = din("w_up", [L, D, DFF])
    w_down_t = din("w_down_t", [L, D // 128, 128, DFF])
    outT = nc.dram_tensor("outT", [D, TQL], F32, kind="ExternalOutput").ap()

    xs = [dscr("xs0", [D, T], F32), dscr("xs1", [D, T], F32)]
    xn = dscr("xn", [D, T], BF16)
    qz = dscr("qz", [AW, T], F32)
    kz = dscr("kz", [KVW, T], F32)
    ud = dscr("ud", [D, T], F32)
    gy = dscr("gy", [D, T], F32)
    sga = dscr("sga", [D, T], F32)
    sgr = dscr("sgr", [D, T], F32)
    vtok = dscr("vtok", [T, KVW], BF16)
    qn = dscr("qn", [AW, T], BF16)
    kn = dscr("kn", [KVW, T], BF16)
    oattn = dscr("oattn", [AW, T], BF16)
    ornn = dscr("ornn", [D, T], BF16)
    tmpA = dscr("tmpA", [D, T], F32)
    merged = dscr("merged", [D, T], BF16)
    xq = dscr("xq", [D, T], BF16)
    mn = dscr("mn", [D, M], BF16)
    xk = dscr("xk", [D, M], BF16)
    xvtok = dscr("xvtok", [M, D], BF16)
    xo = dscr("xo", [D, T], BF16)
    hid = dscr("hid", [DFF, T], BF16)

    from contextlib import ExitStack
    stack = ExitStack()
    P = Prog(nc)
    P.setup_sems(stack)
    NW = 46 * 1024
    big_t = stack.enter_context(nc.sbuf_tensor("big", [128, NW], F32))
    mem = Mem(P, big_t[:], NW)
    banks = []
    pairs = []
    for i in range(4):
        pt = stack.enter_context(nc.psum_tensor("pp%d" % i, [128, 1024], F32))
        pairs.append(Buf("pp%d" % i, apv=pt[:]))
        banks.append(Buf("ps%d" % (2 * i), apv=pt[:, 0:512]))
        banks.append(Buf("ps%d" % (2 * i + 1), apv=pt[:, 512:1024]))
    bank_rr = [0]

    def next_bank():
        b = banks[bank_rr[0] % 8]
        bank_rr[0] += 1
        return b

    class BankRot:
        def __init__(self, idxs):
            self.idxs = idxs
            self.i = 0

        def get(self):
            b = banks[self.idxs[self.i % len(self.idxs)]]
            self.i += 1
            return b

    vecs = mem.alloc("vecs", NV, F32, dma=True)
    P.dma("sp", vecs.apv, vecs_in, w=[vecs], sbuf=vecs)
    flags = mem.alloc("flags", 8, F32, dma=True)
    P.dma("sp", flags.apv[:, 0:2], flags_in, w=[flags], sbuf=flags)
    fA = flags.apv[:, 0:1]
    fB = flags.apv[:, 1:2]
    ones = mem.alloc("ones", 128, BF16)
    P.op("dve", lambda e: e.memset(ones.apv, 1.0), w=[ones])
    perm = mem.alloc("perm", 128, BF16, dma="sw")
    P.dma("pool", perm.apv, perm_in, w=[perm], sbuf=perm)
    ncl = L * 2 * DC
    cl = mem.alloc("cl", ncl, F32)
    cl2 = mem.alloc("cl2", ncl, F32)
    cltmp = mem.alloc("cltmp", ncl, F32)

    def clcol(l, d, c):
        return (l * 2 + d) * DC + c

    for l in range(L):
        for d in range(2):
            c0 = vcols[("lru_lambda", l, d)]
            o0 = clcol(l, d, 0)
            src = vecs.apv[:, c0:c0 + DC]
            t_ = cltmp.apv[:, o0:o0 + DC]
            P.op("act", lambda e, s=src, t=t_: e.activation(out=t, in_=s, func=AF.Exp, scale=-1.0), r=[vecs], w=[cltmp])
            P.op("act", lambda e, t=t_: e.activation(out=t, in_=t, func=AF.Ln, bias=1.0), r=[cltmp], w=[cltmp])
            P.op("dve", lambda e, t=t_, o=cl.apv[:, o0:o0 + DC]: e.tensor_scalar(out=o, in0=t, scalar1=-LRU_C, scalar2=None, op0=ALU.mult), r=[cltmp], w=[cl])
            P.op("dve", lambda e, t=t_, o=cl2.apv[:, o0:o0 + DC]: e.tensor_scalar(out=o, in0=t, scalar1=-2.0 * LRU_C, scalar2=None, op0=ALU.mult), r=[cltmp], w=[cl2])

    def vcol(key, c=0):
        k = vcols[key] + c
        return vecs.apv[:, k:k + 1]

    class Rot:
        def __init__(self, name, n, nelem, dt, dma=True):
            self.bufs = [mem.alloc("%s%d" % (name, i), nelem, dt, dma=dma) for i in range(n)]
            self.i = 0

        def get(self):
            b = self.bufs[self.i % len(self.bufs)]
            self.i += 1
            return b

    ew_rr = [0]

    def ew_eng():
        ew_rr[0] += 1
        return "dve" if ew_rr[0] % 2 else "pool"

    def bcast_mid(ap2d, n_mid):
        p, f = ap2d.ap[0], ap2d.ap[1]
        return bass.AP(ap2d.tensor, ap2d.offset, [list(p), [0, n_mid], list(f)])

    def bcast_last(ap2d, n_last):
        p, f = ap2d.ap[0], ap2d.ap[1]
        return bass.AP(ap2d.tensor, ap2d.offset, [list(p), list(f), [0, n_last]])

    def prep_norm(src, gkey, dst, Tn, tts, final=False):
        mem.push()
        ntt = Tn // tts
        xt_rot = Rot("pn_x", 2 if final else 3, DC * tts, F32)
        sq_rot = Rot("pn_sq", 2, DC * tts, BF16, dma=False)
        out_rot = Rot("pn_o", 2, DC * tts, F32 if final else BF16)
        rs_rot = Rot("pn_rs", 2, tts, F32, dma=False)
        srcv = src.rearrange("(c p) t -> p c t", p=128)
        dstv = dst.rearrange("(c p) t -> p c t", p=128)
        g0 = vcols[gkey]
        g_b = bcast_last(vecs.apv[:, g0:g0 + DC], tts)
        depth = len(xt_rot.bufs)
        xts = {}

        def load(tt_):
            if tt_ < ntt:
                xb_ = xt_rot.get()
                P.dma("sp", xb_.apv.rearrange("p (c t) -> p c t", c=DC), srcv[:, :, tt_ * tts:(tt_ + 1) * tts], w=[xb_], sbuf=xb_)
                xts[tt_] = xb_
        for i_ in range(depth - 1):
            load(i_)
        for tt in range(ntt):
            load(tt + depth - 1)
            xt = xts.pop(tt)
            t0 = tt * tts
            x3 = xt.apv.rearrange("p (c t) -> p c t", c=DC)
            sq = sq_rot.get()
            P.op("act", lambda e, o=sq.apv, i=xt.apv: e.activation(out=o, in_=i, func=AF.Square), r=[xt], w=[sq])
            bk = next_bank()
            for c in range(DC):
                P.op("pe", lambda e, o=bk.apv[:, 0:tts], rh=sq.apv[:, c * tts:(c + 1) * tts], st=(c == 0), sp=(c == DC - 1):
                     e.matmul(o, ones.apv, rh, start=st, stop=sp), r=[sq, ones], w=[bk])
            P.op("dve", lambda e, o=x3, g=g_b: e.tensor_tensor(out=o, in0=o, in1=g, op=ALU.mult), r=[xt, vecs, sq], w=[xt])
            rs = rs_rot.get()
            P.op("act", lambda e, o=rs.apv, i=bk.apv[:, 0:tts]: e.activation(out=o, in_=i, func=AF.Ln, scale=1.0 / D, bias=EPS_T.apv[:, 0:1]), r=[bk, EPS_T], w=[rs])
            P.op("act", lambda e, o=rs.apv: e.activation(out=o, in_=o, func=AF.Exp, scale=-0.5), r=[rs], w=[rs])
            ob = out_rot.get()
            o3 = ob.apv.rearrange("p (c t) -> p c t", c=DC)
            P.op("dve", lambda e, o=o3, i=x3, r_=bcast_mid(rs.apv, DC): e.tensor_tensor(out=o, in0=i, in1=r_, op=ALU.mult), r=[xt, rs], w=[ob])
            P.dma("sp", dstv[:, :, t0:t0 + tts], o3, r=[ob], sbuf=ob)
        mem.pop()

    def linear(src, K, W, colranges, Tn, tts, epi, epi_setup=None, Wtiled=None, epi_pre=None):
        mem.push()
        KC = K // 128
        big_k = KC > 16
        TS = min(Tn, max(tts, ((128 if big_k else 64) * 1024) // (KC * 2)))
        NGW = max(128, 8192 // KC) if Wtiled is None else 128
        NKG = 4 if KC >= 4 else 1
        kpg = KC // NKG
        inb = [mem.alloc("lin_in%d" % i, kpg * TS, BF16, dma=True) for i in range(NKG)]
        wrot = Rot("lin_w", 2 if big_k else 3, KC * NGW, BF16, dma="sw")
        ctx = epi_setup() if epi_setup else None
        srcv = src.rearrange("(c p) t -> p c t", p=128)
        Wv = W.rearrange("(c p) n -> p c n", p=128) if W is not None else None
        groups = []
        for (c0, c1) in colranges:
            n = c0
            while n < c1:
                w_ = min(NGW, c1 - n)
                groups.append((n, w_))
                n += w_
        nts = TS // tts
        setsz = min(4, nts)
        for ts in range(Tn // TS):
            for i in range(NKG):
                P.dma("sp", inb[i].apv.rearrange("p (c t) -> p c t", c=kpg),
                      srcv[:, i * kpg:(i + 1) * kpg, ts * TS:(ts + 1) * TS], w=[inb[i]], sbuf=inb[i])
            for (n0, gw) in groups:
                wb = wrot.get()
                w3 = wb.apv[:, 0:KC * gw].rearrange("p (c n) -> p c n", c=KC)
                if Wtiled is not None:
                    assert gw == 128
                    P.dma("pool", wb.apv[:, 0:KC * 128], Wtiled[n0 // 128], w=[wb], sbuf=wb)
                else:
                    P.dma("pool", w3, Wv[:, :, n0:n0 + gw], w=[wb], sbuf=wb)
                for nci in range(gw // 128):
                    for s0 in range(0, nts, setsz):
                        bks = [next_bank() for _ in range(setsz)]
                        pres = [epi_pre(n0 + nci * 128, ts * TS + (s0 + j) * tts, ctx) if epi_pre else None for j in range(setsz)]
                        for kc in range(KC):
                            ib = inb[kc // kpg]
                            kl = kc % kpg
                            for j in range(setsz):
                                tl = (s0 + j) * tts
                                P.op("pe", lambda e, o=bks[j].apv[:, 0:tts], lh=w3[:, kc, nci * 128:(nci + 1) * 128],
                                     rh=ib.apv[:, kl * TS + tl: kl * TS + tl + tts], st=(kc == 0), sp=(kc == KC - 1):
                                     e.matmul(o, lh, rh, start=st, stop=sp), r=[wb, ib], w=[bks[j]])
                        for j in range(setsz):
                            epi(n0 + nci * 128, ts * TS + (s0 + j) * tts, bks[j], ctx, pres[j])
        mem.pop()

    def linear_tm(src, K, W, c0, ncols, Tn, dst):
        mem.push()
        KC = K // 128
        TS = min(Tn, (64 * 1024) // (KC * 2))
        NKG = 4 if KC >= 4 else 1
        kpg = KC // NKG
        inb = [mem.alloc("ltm_in%d" % i, kpg * TS, BF16, dma=True) for i in range(NKG)]
        GW = min(512, ncols)
        wrot = Rot("ltm_w", 2, KC * GW, BF16, dma="sw")
        orot = Rot("ltm_o", 3, GW, BF16)
        srcv = src.rearrange("(c p) t -> p c t", p=128)
        Wv = W.rearrange("(c p) n -> p c n", p=128)
        for ts in range(Tn // TS):
            for i in range(NKG):
                P.dma("sp", inb[i].apv.rearrange("p (c t) -> p c t", c=kpg),
                      srcv[:, i * kpg:(i + 1) * kpg, ts * TS:(ts + 1) * TS], w=[inb[i]], sbuf=inb[i])
            for g0 in range(0, ncols, GW):
                wb = wrot.get()
                w3 = wb.apv.rearrange("p (c n) -> p c n", c=KC)
                P.dma("pool", w3, Wv[:, :, c0 + g0:c0 + g0 + GW], w=[wb], sbuf=wb)
                for st_ in range(TS // 128):
                    bk = next_bank()
                    for kc in range(KC):
                        ib = inb[kc // kpg]
                        kl = kc % kpg
                        P.op("pe", lambda e, o=bk.apv[:, 0:GW], lh=ib.apv[:, kl * TS + st_ * 128: kl * TS + st_ * 128 + 128],
                             rh=w3[:, kc, :], st=(kc == 0), sp=(kc == KC - 1):
                             e.matmul(o, lh, rh, start=st, stop=sp), r=[wb, ib], w=[bk])
                    ob = orot.get()
                    eng = "act" if st_ % 2 else "dve"
                    if eng == "act":
                        P.op("act", lambda e, o=ob.apv, i=bk.apv[:, 0:GW]: e.activation(out=o, in_=i, func=AF.Copy), r=[bk], w=[ob])
                    else:
                        P.op("dve", lambda e, o=ob.apv, i=bk.apv[:, 0:GW]: e.tensor_copy(out=o, in_=i), r=[bk], w=[ob])
                    tok0 = ts * TS + st_ * 128
                    P.dma("sp", dst[tok0:tok0 + 128, g0:g0 + GW], ob.apv, r=[ob], sbuf=ob)
        mem.pop()

    def epi_store(route, tts):
        def setup():
            return dict(f=Rot("ep_f", 4, tts, F32), b=Rot("ep_b", 4, tts, BF16), k=[0])

        def epi(col0, t0, bk, ctx, pre=None):
            dst, kind = route(col0)
            ctx["k"][0] += 1
            ps = bk.apv[:, 0:tts]
            if kind == "f32":
                ob = ctx["f"].get()
                if ctx["k"][0] % 2:
                    P.op("act", lambda e, o=ob.apv, i=ps: e.activation(out=o, in_=i, func=AF.Copy), r=[bk], w=[ob])
                else:
                    P.op("dve", lambda e, o=ob.apv, i=ps: e.tensor_copy(out=o, in_=i), r=[bk], w=[ob])
            elif kind == "bf16":
                ob = ctx["b"].get()
                if ctx["k"][0] % 2:
                    P.op("act", lambda e, o=ob.apv, i=ps: e.activation(out=o, in_=i, func=AF.Copy), r=[bk], w=[ob])
                else:
                    P.op("dve", lambda e, o=ob.apv, i=ps: e.tensor_copy(out=o, in_=i), r=[bk], w=[ob])
            elif kind == "gelu":
                ob = ctx["f"].get()
                P.op("act", lambda e, o=ob.apv, i=ps: e.activation(out=o, in_=i, func=AF.Gelu), r=[bk], w=[ob])
            elif kind == "sigmoid":
                ob = ctx["f"].get()
                P.op("act", lambda e, o=ob.apv, i=ps: e.activation(out=o, in_=i, func=AF.Sigmoid), r=[bk], w=[ob])
            elif kind == "relu2":
                tb = ctx["f"].get()
                ob = ctx["b"].get()
                P.op("act", lambda e, o=tb.apv, i=ps: e.activation(out=o, in_=i, func=AF.Relu), r=[bk], w=[tb])
                P.op("dve", lambda e, o=ob.apv, i=tb.apv: e.tensor_tensor(out=o, in0=i, in1=i, op=ALU.mult), r=[tb], w=[ob])
            P.dma("sp", dst[:, t0:t0 + tts], ob.apv, r=[ob], sbuf=ob)
        return epi, setup

    def epi_gate(gate_src, dst, tts, addsrc=None, out_dt=F32):
        def setup():
            return dict(g=Rot("eg_g", 8, tts, F32), a=Rot("eg_a", 8, tts, F32) if addsrc is not None else None,
                        o=Rot("eg_o", 3, tts, out_dt))

        def pre(col0, t0, ctx):
            gb = ctx["g"].get()
            P.dma("sp", gb.apv, gate_src[col0:col0 + 128, t0:t0 + tts], w=[gb], sbuf=gb)
            ab = None
            if addsrc is not None:
                ab = ctx["a"].get()
                P.dma("sp", ab.apv, addsrc[col0:col0 + 128, t0:t0 + tts], w=[ab], sbuf=ab)
            return gb, ab

        def epi(col0, t0, bk, ctx, pre_):
            gb, ab = pre_
            ps = bk.apv[:, 0:tts]
            ob = ctx["o"].get()
            if addsrc is None:
                P.op("dve", lambda e, o=ob.apv, i=ps, g=gb.apv: e.tensor_tensor(out=o, in0=i, in1=g, op=ALU.mult), r=[bk, gb], w=[ob])
            else:
                P.op("dve", lambda e, o=gb.apv, i=ps, g=gb.apv: e.tensor_tensor(out=o, in0=i, in1=g, op=ALU.mult), r=[bk, gb], w=[gb])
                P.op("dve", lambda e, o=ob.apv, i=gb.apv, a_=ab.apv: e.tensor_tensor(out=o, in0=i, in1=a_, op=ALU.add), r=[gb, ab], w=[ob])
            P.dma("sp", dst[col0:col0 + 128, t0:t0 + tts], ob.apv, r=[ob], sbuf=ob)
        return epi, setup, pre

    def epi_resid(xold, xnew, tts):
        def setup():
            return dict(x=Rot("er_x", 8, tts, F32))

        def pre(col0, t0, ctx):
            xb = ctx["x"].get()
            P.dma("sp", xb.apv, xold[col0:col0 + 128, t0:t0 + tts], w=[xb], sbuf=xb)
            return xb

        def epi(col0, t0, bk, ctx, xb):
            P.op("dve", lambda e, o=xb.apv, i=bk.apv[:, 0:tts]: e.tensor_tensor(out=o, in0=i, in1=o, op=ALU.add), r=[bk, xb], w=[xb])
            P.dma("sp", xnew[col0:col0 + 128, t0:t0 + tts], xb.apv, r=[xb], sbuf=xb)
        return epi, setup, pre

    def qk_post(l, nq_tiles):
        mem.push()
        NB = 4
        z_rot = Rot("qk_z", 2 * NB, TT, F32)
        sq_rot = Rot("qk_sq", 2 * NB, TT, BF16, dma=False)
        zg_rot = Rot("qk_zg", 2 * NB, TT, F32, dma=False)
        zb_rot = Rot("qk_zb", 2 * NB, TT, BF16, dma=False)
        hr_rot = Rot("qk_hr", 2 * NB, TT, F32, dma=False)
        t1_rot = Rot("qk_t1", 2 * NB, TT, F32, dma=False)
        t2_rot = Rot("qk_t2", 2 * NB, TT, F32, dma=False)
        o_rot = Rot("qk_o", 2 * NB, TT, BF16)
        c_rot = Rot("qk_c", 2, TT, F32)
        s_rot = Rot("qk_s", 2, TT, F32)
        items_q = [(qz, qn, h, ("q_norm_g", l)) for h in range(NQH)]
        items_k = [(kz, kn, h, ("k_norm_g", l)) for h in range(NKV)]
        for tt in range(NTT):
            items = (items_q if tt < nq_tiles else []) + items_k
            t0 = tt * TT
            cb = c_rot.get()
            sb = s_rot.get()
            P.dma("sp", cb.apv, ropeC_in[:, t0:t0 + TT], w=[cb], sbuf=cb)
            P.dma("sp", sb.apv, ropeS_in[:, t0:t0 + TT], w=[sb], sbuf=sb)
            blist = [items[b0:b0 + NB] for b0 in range(0, len(items), NB)]
            zmap = {}

            def load_z(bi_):
                if bi_ < len(blist):
                    zz = [z_rot.get() for _ in range(len(blist[bi_]))]
                    for i_, (srcz_, dstn_, h_, gk_) in enumerate(blist[bi_]):
                        P.dma("sp", zz[i_].apv, srcz_[h_ * 128:(h_ + 1) * 128, t0:t0 + TT], w=[zz[i_]], sbuf=zz[i_])
                    zmap[bi_] = zz
            load_z(0)
            for bi_, batch in enumerate(blist):
                n = len(batch)
                load_z(bi_ + 1)
                zs = zmap.pop(bi_)
                sqs = [sq_rot.get() for _ in range(n)]
                for i in range(n):
                    P.op("act", lambda e, o=sqs[i].apv, i_=zs[i].apv: e.activation(out=o, in_=i_, func=AF.Square), r=[zs[i]], w=[sqs[i]])
                for i in range(n):
                    P.op("pe", lambda e, o=banks[i].apv[:, 0:TT], rh=sqs[i].apv: e.matmul(o, ones.apv, rh, start=True, stop=True), r=[sqs[i], ones], w=[banks[i]])
                zbs = [zb_rot.get() for _ in range(n)]
                for i, (srcz, dstn, h, gkey) in enumerate(batch):
                    P.op("act", lambda e, o=zbs[i].apv, i_=zs[i].apv, g=vcol(gkey): e.activation(out=o, in_=i_, func=AF.Identity, scale=g), r=[zs[i], vecs], w=[zbs[i]])
                for i in range(n):
                    P.op("pe", lambda e, o=banks[4 + i].apv[:, 0:TT], rh=zbs[i].apv: e.matmul(o, perm.apv, rh, start=True, stop=True), r=[zbs[i], perm], w=[banks[4 + i]])
                zgs = [zg_rot.get() for _ in range(n)]
                for i, (srcz, dstn, h, gkey) in enumerate(batch):
                    P.op("act", lambda e, o=zgs[i].apv, i_=zs[i].apv, g=vcol(gkey): e.activation(out=o, in_=i_, func=AF.Identity, scale=g), r=[zs[i], vecs], w=[zgs[i]])
                hrs = [hr_rot.get() for _ in range(n)]
                for i in range(n):
                    P.op("act", lambda e, o=hrs[i].apv, i_=banks[i].apv[:, 0:TT]: e.activation(out=o, in_=i_, func=AF.Ln, scale=1.0 / 128, bias=EPS_T.apv[:, 0:1]), r=[banks[i], EPS_T], w=[hrs[i]])
                for i in range(n):
                    P.op("act", lambda e, o=hrs[i].apv: e.activation(out=o, in_=o, func=AF.Exp, scale=-0.5), r=[hrs[i]], w=[hrs[i]])
                t1s = [t1_rot.get() for _ in range(n)]
                t2s = [t2_rot.get() for _ in range(n)]
                for i in range(n):
                    P.op("pool", lambda e, o=t1s[i].apv, i_=zgs[i].apv, c=cb.apv: e.tensor_tensor(out=o, in0=i_, in1=c, op=ALU.mult), r=[zgs[i], cb], w=[t1s[i]])
                for i in range(n):
                    P.op("dve", lambda e, o=t2s[i].apv, i_=banks[4 + i].apv[:, 0:TT], s_=sb.apv: e.tensor_tensor(out=o, in0=i_, in1=s_, op=ALU.mult), r=[banks[4 + i], sb], w=[t2s[i]])
                for i in range(n):
                    P.op("dve", lambda e, o=t2s[i].apv, a_=t1s[i].apv, b_=t2s[i].apv: e.tensor_tensor(out=o, in0=a_, in1=b_, op=ALU.add), r=[t1s[i], t2s[i]], w=[t2s[i]])
                for i, (srcz, dstn, h, gkey) in enumerate(batch):
                    ob = o_rot.get()
                    P.op("dve", lambda e, o=ob.apv, a_=t2s[i].apv, b_=hrs[i].apv: e.tensor_tensor(out=o, in0=a_, in1=b_, op=ALU.mult), r=[t2s[i], hrs[i]], w=[ob])
                    P.dma("sp", dstn[h * 128:(h + 1) * 128, t0:t0 + TT], ob.apv, r=[ob], sbuf=ob)
        mem.pop()

    def attention(n_qtiles):
        mem.push()
        SC = S // 128
        assert SC % 2 == 0
        NP = SC // 2
        scale = float(HD) ** -0.5
        kT = [mem.alloc("at_k%d" % h, S, BF16, dma=True) for h in range(NKV)]
        for h in range(NKV):
            P.dma("sp", kT[h].apv, kn[h * 128:(h + 1) * 128, :], w=[kT[h]], sbuf=kT[h])
        NVG = 4 if SC >= 4 else 1
        spg = SC // NVG
        vb = [mem.alloc("at_v%d" % i, spg * KVW, BF16, dma=True) for i in range(NVG)]
        vv = vtok.rearrange("(c p) n -> p c n", p=128)
        for i in range(NVG):
            P.dma("sp", vb[i].apv.rearrange("p (c n) -> p c n", c=spg), vv[:, i * spg:(i + 1) * spg, :], w=[vb[i]], sbuf=vb[i])
        q_rot = Rot("at_q", 2, T, BF16)
        p_rot = Rot("at_p", 3, 2 * TT, BF16, dma=False)
        rd_rot = Rot("at_rd", 2, TT, F32, dma=False)
        o_rot = Rot("at_o", 2, TT, BF16)
        bro = BankRot([0, 1])
        brd = BankRot([2, 3])
        spair = [pairs[2], pairs[3]]
        spi = [0]
        qbs = {}

        def load_q(hq_):
            if hq_ < NQH:
                qb_ = q_rot.get()
                P.dma("sp", qb_.apv[:, 0:n_qtiles * TT], qn[hq_ * 128:(hq_ + 1) * 128, 0:n_qtiles * TT], w=[qb_], sbuf=qb_)
                qbs[hq_] = qb_
        load_q(0)
        for hq in range(NQH):
            hk = hq // GROUP
            load_q(hq + 1)
            qb = qbs.pop(hq)
            for qt in range(n_qtiles):
                qs = qb.apv[:, qt * TT:(qt + 1) * TT]
                bo = bro.get()
                bd = brd.get()
                sp_ = [None] * NP

                def score(jp):
                    pr = spair[spi[0] % 2]
                    spi[0] += 1
                    sp_[jp] = pr
                    for hh in range(2):
                        j = 2 * jp + hh
                        P.op("pe", lambda e, o=pr.apv[:, hh * 512: hh * 512 + TT], lh=kT[hk].apv[:, j * 128:(j + 1) * 128], rh=qs:
                             e.matmul(o, lh, rh, start=True, stop=True), r=[kT[hk], qb], w=[pr])
                score(0)
                for jp in range(NP):
                    if jp + 1 < NP:
                        score(jp + 1)
                    pb = p_rot.get()
                    pr = sp_[jp]
                    if TT == 512:
                        P.op("act", lambda e, o=pb.apv, i=pr.apv: e.activation(out=o, in_=i, func=AF.Exp, scale=scale), r=[pr], w=[pb])
                    else:
                        for hh in range(2):
                            P.op("act", lambda e, o=pb.apv[:, hh * TT:(hh + 1) * TT], i=pr.apv[:, hh * 512: hh * 512 + TT]:
                                 e.activation(out=o, in_=i, func=AF.Exp, scale=scale), r=[pr], w=[pb])
                    for hh in range(2):
                        j = 2 * jp + hh
                        vbuf = vb[j // spg]
                        vl = (j % spg) * KVW + hk * 128
                        ph = pb.apv[:, hh * TT:(hh + 1) * TT]
                        P.op("pe", lambda e, o=bo.apv[:, 0:TT], lh=vbuf.apv[:, vl:vl + 128], rh=ph, st=(j == 0), sp=(j == SC - 1):
                             e.matmul(o, lh, rh, start=st, stop=sp), r=[vbuf, pb], w=[bo])
                        P.op("pe", lambda e, o=bd.apv[:, 0:TT], rh=ph, st=(j == 0), sp=(j == SC - 1):
                             e.matmul(o, ones.apv, rh, start=st, stop=sp), r=[ones, pb], w=[bd])
                rd = rd_rot.get()
                P.op("dve", lambda e, o=rd.apv, i=bd.apv[:, 0:TT]: e.reciprocal(out=o, in_=i), r=[bd], w=[rd])
                ob = o_rot.get()
                P.op("dve", lambda e, o=ob.apv, i=bo.apv[:, 0:TT], r_=rd.apv: e.tensor_tensor(out=o, in0=i, in1=r_, op=ALU.mult), r=[bo, rd], w=[ob])
                P.dma("sp", oattn[hq * 128:(hq + 1) * 128, qt * TT:(qt + 1) * TT], ob.apv, r=[ob], sbuf=ob)
        mem.pop()

    def lru(l, out_T):
        mem.push()
        GB = min(4, NTT)
        GW_ = GB * TT
        upA = mem.alloc("lr_uA", H + 3, F32, dma=True)
        upB = mem.alloc("lr_uB", H + 3, F32, dma=True)
        ufs = [mem.alloc("lr_uf%d" % i, T, F32) for i in range(2)]
        ubs = [mem.alloc("lr_ub%d" % i, T, BF16, dma=True) for i in range(2)]
        ufh = [[mem.view("lr_uf%d%d" % (i, h), ufs[i], h * H, (h + 1) * H) for h in range(2)] for i in range(2)]
        ubh = [[mem.view("lr_ub%d%d" % (i, h), ubs[i], h * H, (h + 1) * H) for h in range(2)] for i in range(2)]
        afull = mem.alloc("lr_a", T, F32)
        bfull = mem.alloc("lr_b", T, F32)
        hf = mem.alloc("lr_hf", T, F32)
        hb = mem.alloc("lr_hb", T, F32)
        gyb = mem.alloc("lr_gy", out_T, F32, dma=True)
        ini = mem.alloc("lr_ini", 8, F32)
        nbat = NTT // GB
        a_t = [mem.view("lr_a%d" % i, afull, i * GW_, (i + 1) * GW_) for i in range(nbat)]
        b_t = [mem.view("lr_b%d" % i, bfull, i * GW_, (i + 1) * GW_) for i in range(nbat)]
        wg = [[mem.alloc("lr_w%d%d" % (d, g), 128, BF16, dma="sw") for g in range(2)] for d in range(2)]
        r_rot = Rot("lr_r", 2, GW_, F32, dma=False)
        i_rot = Rot("lr_i", 1, GW_, F32, dma=False)
        e_rot = Rot("lr_e", 1, GW_, F32, dma=False)
        npair = max(1, GB // 2)

        def tsmul(o, i, f, rbufs, wbufs):
            P.op("dve", lambda e, o=o, i=i, f=f: e.tensor_scalar(out=o, in0=i, scalar1=f, scalar2=None, op0=ALU.mult), r=rbufs + [flags], w=wbufs)

        def scan(o, a_, b_, init, rbufs, wbufs, rev=False):
            if rev:
                o, a_, b_ = rev_ap(o), rev_ap(a_), rev_ap(b_)
            P.op("dve", lambda e, o=o, a_=a_, b_=b_, init=init: e.tensor_tensor_scan(out=o, data0=a_, data1=b_, initial=init, op0=ALU.mult, op1=ALU.add),
                 r=rbufs, w=wbufs)

        def load_u(c):
            P.dma("sp", upA.apv[:, 2:H + 2], ud[c * 128:(c + 1) * 128, 0:H], w=[upA], sbuf=upA)
            P.dma("sp", upB.apv[:, 2:H + 2], ud[c * 128:(c + 1) * 128, H:T], w=[upB], sbuf=upB)

        def conv_act1(c, s_):
            tsmul(upA.apv[:, 0:2], upB.apv[:, H:H + 2], fB, [upB], [upA])
            tsmul(upA.apv[:, H + 2:H + 3], upB.apv[:, 2:3], fA, [upB], [upA])
            tsmul(upB.apv[:, 0:2], upA.apv[:, H:H + 2], fA, [upA], [upB])
            tsmul(upB.apv[:, H + 2:H + 3], upA.apv[:, 2:3], fB, [upA], [upB])
            for h_, up_ in enumerate((upA, upB)):
                P.op("act", lambda e, o=ufh[s_][h_].apv, i=up_.apv[:, 0:H], w0=vcol(("conv_w", l, 0), c), b=vcol(("conv_b", l), c):
                     e.activation(out=o, in_=i, func=AF.Identity, scale=w0, bias=b), r=[up_, vecs], w=[ufh[s_][h_]])

        def conv_dve(c, s_):
            for h_, up_ in enumerate((upA, upB)):
                for k in range(1, 4):
                    P.op("dve", lambda e, o=ufh[s_][h_].apv, i=up_.apv[:, k:k + H], wk=vcol(("conv_w", l, k), c):
                         e.scalar_tensor_tensor(out=o, in0=i, scalar=wk, in1=o, op0=ALU.mult, op1=ALU.add), r=[up_, ufh[s_][h_], vecs], w=[ufh[s_][h_]])

        def conv_act2(c, s_):
            for h_ in range(2):
                P.op("act", lambda e, o=ubh[s_][h_].apv, i=ufh[s_][h_].apv: e.activation(out=o, in_=i, func=AF.Copy), r=[ufh[s_][h_]], w=[ubh[s_][h_]])

        def gates(c, d, s_):
            k = clcol(l, d, c)
            uf_, ub_ = ufs[s_], ubs[s_]
            for bi in range(nbat):
                c0 = bi * GW_
                for ti in range(GB):
                    pr_r = pairs[ti // 2]
                    pr_i = pairs[2 + ti // 2]
                    hh = ti % 2
                    rh = ub_.apv[:, c0 + ti * TT: c0 + (ti + 1) * TT]
                    P.op("pe", lambda e, o=pr_r.apv[:, hh * 512: hh * 512 + TT], lh=wg[d][0].apv, rh=rh: e.matmul(o, lh, rh, start=True, stop=True), r=[wg[d][0]] + ubh[s_], w=[pr_r])
                    P.op("pe", lambda e, o=pr_i.apv[:, hh * 512: hh * 512 + TT], lh=wg[d][1].apv, rh=rh: e.matmul(o, lh, rh, start=True, stop=True), r=[wg[d][1]] + ubh[s_], w=[pr_i])
                rb = r_rot.get()
                ib_ = i_rot.get()
                eb = e_rot.get()
                for (dst_, pbase, bkey) in ((rb, 0, ("lru_b_r", l, d)), (ib_, 2, ("lru_b_i", l, d))):
                    if TT == 512 and GB >= 2:
                        for pi in range(npair):
                            P.op("act", lambda e, o=dst_.apv[:, pi * 1024:(pi + 1) * 1024], i=pairs[pbase + pi].apv, b=vcol(bkey, c):
                                 e.activation(out=o, in_=i, func=AF.Sigmoid, bias=b), r=[pairs[pbase + pi], vecs], w=[dst_])
                    else:
                        for ti in range(GB):
                            P.op("act", lambda e, o=dst_.apv[:, ti * TT:(ti + 1) * TT], i=pairs[pbase + ti // 2].apv[:, (ti % 2) * 512:(ti % 2) * 512 + TT], b=vcol(bkey, c):
                                 e.activation(out=o, in_=i, func=AF.Sigmoid, bias=b), r=[pairs[pbase + ti // 2], vecs], w=[dst_])
                P.op("act", lambda e, o=a_t[bi].apv, i=rb.apv, sc_=cl.apv[:, k:k + 1]: e.activation(out=o, in_=i, func=AF.Exp, scale=sc_), r=[rb, cl], w=[a_t[bi]])
                P.op("act", lambda e, o=eb.apv, i=rb.apv, sc_=cl2.apv[:, k:k + 1]: e.activation(out=o, in_=i, func=AF.Exp, scale=sc_), r=[rb, cl2], w=[eb])
                P.op("act", lambda e, o=eb.apv: e.activation(out=o, in_=o, func=AF.Sqrt, scale=-1.0, bias=ONE_T.apv[:, 0:1]), r=[eb, ONE_T], w=[eb])
                P.op("dve", lambda e, o=ib_.apv, u_=uf_.apv[:, c0:c0 + GW_]: e.tensor_tensor(out=o, in0=o, in1=u_, op=ALU.mult), r=[ib_] + ufh[s_], w=[ib_])
                P.op("dve", lambda e, o=b_t[bi].apv, i=ib_.apv, s2=eb.apv: e.tensor_tensor(out=o, in0=i, in1=s2, op=ALU.mult), r=[ib_, eb], w=[b_t[bi]])

        def scans(d):
            ab_ = a_t + b_t
            aA, aB = afull.apv[:, 0:H], afull.apv[:, H:T]
            bA, bB = bfull.apv[:, 0:H], bfull.apv[:, H:T]
            if d == 0:
                scan(hf.apv[:, H:T], aB, bB, 0.0, ab_, [hf])
                tsmul(ini.apv[:, 0:1], hf.apv[:, T - 1:T], fB, [hf], [ini])
                scan(hf.apv[:, 0:H], aA, bA, ini.apv[:, 0:1], ab_ + [ini], [hf])
                if out_T > H:
                    tsmul(ini.apv[:, 1:2], hf.apv[:, H - 1:H], fA, [hf], [ini])
                    scan(hf.apv[:, H:T], aB, bB, ini.apv[:, 1:2], ab_ + [ini], [hf])
            else:
                scan(hb.apv[:, 0:H], aA, bA, 0.0, ab_, [hb], rev=True)
                tsmul(ini.apv[:, 2:3], hb.apv[:, 0:1], fB, [hb], [ini])
                scan(hb.apv[:, H:T], aB, bB, ini.apv[:, 2:3], ab_ + [ini], [hb], rev=True)
                tsmul(ini.apv[:, 3:4], hb.apv[:, H:H + 1], fA, [hb], [ini])
                scan(hb.apv[:, 0:H], aA, bA, ini.apv[:, 3:4], ab_ + [ini], [hb], rev=True)

        load_u(0)
        conv_act1(0, 0)
        conv_dve(0, 0)
        conv_act2(0, 0)
        for c in range(DC):
            s_ = c % 2
            n_ = 1 - s_
            more = c + 1 < DC
            P.dma("sp", gyb.apv, gy[c * 128:(c + 1) * 128, 0:out_T], w=[gyb], sbuf=gyb)
            for d in range(2):
                P.dma("pool", wg[d][0].apv, lru_w_r[l, d, c], w=[wg[d][0]], sbuf=wg[d][0])
                P.dma("pool", wg[d][1].apv, lru_w_i[l, d, c], w=[wg[d][1]], sbuf=wg[d][1])
            if more:
                load_u(c + 1)
            gates(c, 0, s_)
            if more:
                conv_act1(c + 1, n_)
            scans(0)
            if more:
                conv_dve(c + 1, n_)
            gates(c, 1, s_)
            if more:
                conv_act2(c + 1, n_)
            scans(1)
            P.op("dve", lambda e, o=hf.apv[:, 0:out_T], b=hb.apv[:, 0:out_T]: e.tensor_tensor(out=o, in0=o, in1=b, op=ALU.add), r=[hf, hb], w=[hf])
            ob = ubs[s_]
            P.op("dve", lambda e, o=ob.apv[:, 0:out_T], a=hf.apv[:, 0:out_T], g=gyb.apv: e.tensor_tensor(out=o, in0=a, in1=g, op=ALU.mult), r=[hf, gyb] + ubh[s_], w=[ob] + ubh[s_])
            P.dma("sp", ornn[c * 128:(c + 1) * 128, 0:out_T], ob.apv[:, 0:out_T], r=[ob] + ubh[s_], sbuf=ob)
        mem.pop()

    def cross_attention(n_qtiles):
        mem.push()
        MC = M // 128
        scale = float(XHD) ** -0.5
        kb = mem.alloc("ca_k", DC * M, BF16, dma=True)
        P.dma("sp", kb.apv.rearrange("p (c m) -> p c m", c=DC), xk.rearrange("(c p) m -> p c m", p=128), w=[kb], sbuf=kb)
        vb_ = mem.alloc("ca_v", MC * D, BF16, dma=True)
        P.dma("sp", vb_.apv.rearrange("p (c n) -> p c n", c=MC), xvtok.rearrange("(c p) n -> p c n", p=128), w=[vb_], sbuf=vb_)
        q_rot = Rot("ca_q", 2, DC * TT, BF16)
        p_rot = Rot("ca_p", 2 * MC, TT, BF16, dma=False)
        rd_rot = Rot("ca_rd", 2, TT, F32, dma=False)
        o_rot = Rot("ca_o", 3, TT, BF16)
        xqv = xq.rearrange("(c p) t -> p c t", p=128)
        brs = BankRot([0, 1, 2, 3])
        brd = BankRot([4, 5])
        bro = BankRot([6, 7])
        cqs = {}

        def load_cq(qt_):
            if qt_ < n_qtiles:
                qb_ = q_rot.get()
                P.dma("sp", qb_.apv.rearrange("p (c t) -> p c t", c=DC), xqv[:, :, qt_ * TT:(qt_ + 1) * TT], w=[qb_], sbuf=qb_)
                cqs[qt_] = qb_
        load_cq(0)
        for qt in range(n_qtiles):
            t0 = qt * TT
            load_cq(qt + 1)
            qb = cqs.pop(qt)
            for h in range(NXH):
                pbs = []
                for mc in range(MC):
                    bs = brs.get()
                    for dc in range(XDC):
                        ch = h * XDC + dc
                        P.op("pe", lambda e, o=bs.apv[:, 0:TT], lh=kb.apv[:, ch * M + mc * 128: ch * M + mc * 128 + 128],
                             rh=qb.apv[:, ch * TT:(ch + 1) * TT], st=(dc == 0), sp=(dc == XDC - 1):
                             e.matmul(o, lh, rh, start=st, stop=sp), r=[kb, qb], w=[bs])
                    pb = p_rot.get()
                    P.op("act", lambda e, o=pb.apv, i=bs.apv[:, 0:TT]: e.activation(out=o, in_=i, func=AF.Exp, scale=scale), r=[bs], w=[pb])
                    pbs.append(pb)
                bd = brd.get()
                for mc in range(MC):
                    P.op("pe", lambda e, o=bd.apv[:, 0:TT], rh=pbs[mc].apv, st=(mc == 0), sp=(mc == MC - 1):
                         e.matmul(o, ones.apv, rh, start=st, stop=sp), r=[ones, pbs[mc]], w=[bd])
                rd = rd_rot.get()
                P.op("dve", lambda e, o=rd.apv, i=bd.apv[:, 0:TT]: e.reciprocal(out=o, in_=i), r=[bd], w=[rd])
                for dc in range(XDC):
                    ch = h * XDC + dc
                    bo = bro.get()
                    for mc in range(MC):
                        P.op("pe", lambda e, o=bo.apv[:, 0:TT], lh=vb_.apv[:, mc * D + ch * 128: mc * D + ch * 128 + 128], rh=pbs[mc].apv,
                             st=(mc == 0), sp=(mc == MC - 1): e.matmul(o, lh, rh, start=st, stop=sp), r=[vb_, pbs[mc]], w=[bo])
                    ob = o_rot.get()
                    P.op("dve", lambda e, o=ob.apv, i=bo.apv[:, 0:TT], r_=rd.apv: e.tensor_tensor(out=o, in0=i, in1=r_, op=ALU.mult), r=[bo, rd], w=[ob])
                    P.dma("sp", xo[ch * 128:(ch + 1) * 128, t0:t0 + TT], ob.apv, r=[ob], sbuf=ob)
        mem.pop()

    EPS_T = mem.alloc("eps_t", 8, F32)
    ONE_T = mem.alloc("one_t", 8, F32)
    P.op("dve", lambda e: e.memset(EPS_T.apv, EPS), w=[EPS_T])
    P.op("dve", lambda e: e.memset(ONE_T.apv, 1.0), w=[ONE_T])
    P.barrier()

    xcur = xT_in
    xi = 0
    for l in range(L):
        Tq = TQL if l == L - 1 else T
        NQT = Tq // TT
        prep_norm(xcur, ("mix_norm_g", l), xn, T, PT)

        def route_in(col0):
            if col0 < o_k:
                return qz[col0 - o_q: col0 - o_q + 128], "f32"
            if col0 < o_v:
                return kz[col0 - o_k: col0 - o_k + 128], "f32"
            if col0 < o_y:
                return ud[col0 - o_u: col0 - o_u + 128], "f32"
            if col0 < o_ga:
                return gy[col0 - o_y: col0 - o_y + 128], "gelu"
            if col0 < o_gr:
                return sga[col0 - o_ga: col0 - o_ga + 128], "sigmoid"
            return sgr[col0 - o_gr: col0 - o_gr + 128], "sigmoid"
        ep, su = epi_store(route_in, TT)
        if Tq == T:
            linear(xn, D, w_in[l], [(0, o_v), (o_u, NIN)], T, TT, ep, su)
        else:
            linear(xn, D, w_in[l], [(o_k, o_v), (o_u, o_y)], T, TT, ep, su)
            linear(xn, D, w_in[l], [(0, o_k), (o_y, NIN)], Tq, TT, ep, su)
        linear_tm(xn, D, w_in[l], o_v, KVW, T, vtok)
        qk_post(l, NQT)
        attention(NQT)
        lru(l, Tq)
        ep, su, pr = epi_gate(sga, tmpA, TT)
        linear(oattn, AW, w_ab[l], [(0, D)], Tq, TT, ep, su, epi_pre=pr)
        ep, su, pr = epi_gate(sgr, merged, TT, addsrc=tmpA, out_dt=BF16)
        linear(ornn, D, w_rb[l], [(0, D)], Tq, TT, ep, su, epi_pre=pr)
        xnext = xs[xi % 2]; xi += 1
        ep, su, pr = epi_resid(xcur, xnext, TT)
        linear(merged, D, w_mo[l], [(0, D)], Tq, TT, ep, su, epi_pre=pr)
        xcur = xnext
        prep_norm(xcur, ("cross_norm_g", l), xn, Tq, PT)
        ep, su = epi_store(lambda c0: (xq[c0:c0 + 128], "bf16"), TT)
        linear(xn, D, w_xq[l], [(0, D)], Tq, TT, ep, su)
        prep_norm(memT_in, ("mem_norm_g", l), mn, M, min(MT, PT))
        ep, su = epi_store(lambda c0: (xk[c0:c0 + 128], "bf16"), MT)
        linear(mn, D, w_xkv[l], [(0, D)], M, MT, ep, su)
        linear_tm(mn, D, w_xkv[l], D, D, M, xvtok)
        cross_attention(NQT)
        xnext = xs[xi % 2]; xi += 1
        ep, su, pr = epi_resid(xcur, xnext, TT)
        linear(xo, D, w_xo[l], [(0, D)], Tq, TT, ep, su, epi_pre=pr)
        xcur = xnext
        prep_norm(xcur, ("mlp_norm_g", l), xn, Tq, PT)
        ep, su = epi_store(lambda c0: (hid[c0:c0 + 128], "relu2"), TT)
        linear(xn, D, w_up[l], [(0, DFF)], Tq, TT, ep, su)
        xnext = xs[xi % 2]; xi += 1
        ep, su, pr = epi_resid(xcur, xnext, TT)
        linear(hid, DFF, None, [(0, D)], Tq, TT, ep, su, Wtiled=w_down_t[l], epi_pre=pr)
        xcur = xnext
    prep_norm(xcur, ("final_norm_g",), outT, TQL, PT, final=True)
    P.barrier()

    P.finalize()
    with nc.Block() as block:
        @block.sync
        def _(e):
            P.emit("sp", e)

        @block.tensor
        def _(e):
            P.emit("pe", e)

        @block.scalar
        def _(e):
            P.emit("act", e)

        @block.vector
        def _(e):
            P.emit("dve", e)

        @block.gpsimd
        def _(e):
            P.emit("pool", e)
    stack.close()
    return nc, P


def rope_tables(cfg):
    S, GW = cfg["S"], cfg["GRID_W"]
    rows_n = S // GW
    row = np.repeat(np.arange(rows_n, dtype=np.float32), GW)
    col = np.tile(np.arange(GW, dtype=np.float32), rows_n)
    n_freq = 32
    inv = (np.float32(ROPE_THETA) ** (-np.arange(n_freq, dtype=np.float32) / np.float32(n_freq))).astype(np.float32)
    ang_r = (row[:, None] * inv[None, :]).astype(np.float32)
    ang_c = (col[:, None] * inv[None, :]).astype(np.float32)
    cr, sr, cc, sc = np.cos(ang_r), np.sin(ang_r), np.cos(ang_c), np.sin(ang_c)
    C = np.concatenate([cr, cr, cc, cc], axis=1).T
    Sg = np.concatenate([-sr, sr, -sc, sc], axis=1).T
    return np.ascontiguousarray(C, dtype=np.float32), np.ascontiguousarray(Sg, dtype=np.float32)


def perm_matrix():
    Pm = np.zeros((128, 128), np.float32)
    for blk in range(2):
        b = blk * 64
        for i in range(32):
            Pm[b + i, b + 32 + i] = 1.0
            Pm[b + 32 + i, b + i] = 1.0
    return Pm


def pack_vecs(cfg, inp):
    cols, NV = vec_layout(cfg)
    L, D = cfg["L"], cfg["D"]
    DC = D // 128
    V = np.zeros((128, NV), np.float32)

    def put(key, vec):
        v = np.asarray(vec, np.float32)
        k = v.size // 128
        V[:, cols[key]:cols[key] + k] = v.reshape(k, 128).T

    for l in range(L):
        for nm in ("mix_norm_g", "cross_norm_g", "mem_norm_g", "mlp_norm_g"):
            put((nm, l), inp[nm][l])
        put(("q_norm_g", l), inp["q_norm_g"][l])
        put(("k_norm_g", l), inp["k_norm_g"][l])
        for k in range(4):
            put(("conv_w", l, k), inp["conv_w"][l, k])
        put(("conv_b", l), inp["conv_b"][l])
        for d in range(2):
            put(("lru_b_r", l, d), inp["lru_b_r"][l, d])
            put(("lru_b_i", l, d), inp["lru_b_i"][l, d])
            put(("lru_lambda", l, d), inp["lru_lambda"][l, d])
    put(("final_norm_g",), inp["final_norm_g"])
    return V


def make_in_maps(cfg, inp, cores):
    C, Sg = rope_tables(cfg)
    Pm = perm_matrix()
    V = pack_vecs(cfg, inp)
    S = cfg["S"]
    Hh = S // 2
    shared = {"vecs": V, "perm": Pm}
    for k in ("w_in", "lru_w_r", "lru_w_i", "w_attn_branch", "w_rnn_branch", "w_mix_out", "w_xq", "w_xkv", "w_xo", "w_up"):
        shared[k] = np.ascontiguousarray(np.asarray(inp[k], np.float32))
    wd = np.asarray(inp["w_down"], np.float32)
    L_, DFF_, D_ = wd.shape
    shared["w_down_t"] = np.ascontiguousarray(
        wd.reshape(L_, DFF_ // 128, 128, D_ // 128, 128).transpose(0, 3, 2, 1, 4).reshape(L_, D_ // 128, 128, DFF_))
    maps = []
    for (b, hf) in cores:
        idx = np.concatenate([np.arange(hf * Hh, hf * Hh + Hh), np.arange((1 - hf) * Hh, (1 - hf) * Hh + Hh)])
        m = dict(shared)
        m["xT"] = np.ascontiguousarray(np.asarray(inp["x"][b], np.float32)[idx].T)
        m["memT"] = np.ascontiguousarray(np.asarray(inp["mem"][b], np.float32).T)
        m["ropeC"] = np.ascontiguousarray(C[:, idx])
        m["ropeS"] = np.ascontiguousarray(Sg[:, idx])
        fl = np.zeros((128, 2), np.float32)
        fl[:, 0] = 1.0 - hf
        fl[:, 1] = float(hf)
        m["flags"] = fl
        maps.append(m)
    return maps


_NC_CACHE = {}


def kernel(**inputs):
    cfg = FULL_CFG
    B = inputs["x"].shape[0]
    S = cfg["S"]
    if "nc" not in _NC_CACHE:
        _NC_CACHE["nc"] = build_program(cfg)[0]
    nc = _NC_CACHE["nc"]
    if cfg.get("SPLIT2"):
        cores = [(b, hf) for b in range(B) for hf in range(2)]
    else:
        cores = [(b, 0) for b in range(B)]
    in_maps = make_in_maps(cfg, inputs, cores)
    res = run_bass_kernel_spmd(nc, in_maps, core_ids=list(range(len(cores))))
    out = np.zeros((B, S, cfg["D"]), np.float32)
    Hh = S // 2
    for (b, hf), r in zip(cores, res.results):
        o = np.ascontiguousarray(r["outT"].T)
        if cfg.get("SPLIT2"):
            out[b, hf * Hh:(hf + 1) * Hh] = o
        else:
            out[b] = o
    return out
```
